# Optimizing a Trainium2 kernel written in Bass

```python
import math
import jax
import jax.numpy as jnp
from jax import lax
import numpy as np

D_MODEL = 1024
BATCH = 16
SEQ = 2048
DEPTH = 1

PLE_DIM = 256
D_FF = 2816
SSM_GROUP_CH = 16
SSM_GROUPS = 32
D_SSM = SSM_GROUPS * SSM_GROUP_CH
SSM_STATE = 64
GMLP_HEADS = 8
GMLP_HEAD_DIM = 64
D_GMLP = GMLP_HEADS * GMLP_HEAD_DIM
CHUNK = 128
D_IN = D_SSM + 2 * D_GMLP + 2 * D_MODEL
LN_EPS = 1e-5
DEEPNORM_ALPHA = (2.0 * DEPTH) ** 0.25
DEEPNORM_BETA = (8.0 * DEPTH) ** -0.25

kernel_name = "hybrid_s5_gmlp_macaron_deepnorm"


def _layer_norm(x, g, b):
    xf = x.astype(jnp.float32)
    mu = jnp.mean(xf, axis=-1, keepdims=True)
    var = jnp.mean(jnp.square(xf - mu), axis=-1, keepdims=True)
    y = (xf - mu) * lax.rsqrt(var + LN_EPS) * g.astype(jnp.float32) + b.astype(jnp.float32)
    return y.astype(x.dtype)


def _swiglu(x, w_in, w_out):
    h = x @ w_in
    gate, up = jnp.split(h, 2, axis=-1)
    return (jax.nn.silu(gate) * up) @ w_out


def _complex_affine_combine(e1, e2):
    a1r, a1i, b1r, b1i = e1
    a2r, a2i, b2r, b2i = e2
    ar = a2r * a1r - a2i * a1i
    ai = a2r * a1i + a2i * a1r
    br = a2r * b1r - a2i * b1i + b2r
    bi = a2r * b1i + a2i * b1r + b2i
    return (ar, ai, br, bi)


def _s5_branch(u, lam_re, lam_im, log_dt, b_re, b_im, c_re, c_im, d_skip, glu_w, glu_b):
    bsz, seq, _ = u.shape
    f32 = jnp.float32
    ug = u.reshape(bsz, seq, SSM_GROUPS, SSM_GROUP_CH).astype(f32)
    dt = jnp.exp(log_dt.astype(f32))[:, None]
    lre = lam_re.astype(f32)
    lim = lam_im.astype(f32)
    mag = jnp.exp(lre * dt)
    ab_re = mag * jnp.cos(lim * dt)
    ab_im = mag * jnp.sin(lim * dt)
    nr = ab_re - 1.0
    ni = ab_im
    den = lre * lre + lim * lim
    coef_re = ((nr * lre + ni * lim) / den)[..., None]
    coef_im = ((ni * lre - nr * lim) / den)[..., None]
    bre = b_re.astype(f32)
    bim = b_im.astype(f32)
    bb_re = coef_re * bre - coef_im * bim
    bb_im = coef_re * bim + coef_im * bre
    bu_re = jnp.einsum('blgi,gpi->blgp', ug, bb_re)
    bu_im = jnp.einsum('blgi,gpi->blgp', ug, bb_im)
    a_re = jnp.broadcast_to(ab_re, bu_re.shape)
    a_im = jnp.broadcast_to(ab_im, bu_im.shape)
    _, _, h_re, h_im = lax.associative_scan(
        _complex_affine_combine, (a_re, a_im, bu_re, bu_im), axis=1)
    y = (jnp.einsum('gip,blgp->blgi', c_re.astype(f32), h_re)
         - jnp.einsum('gip,blgp->blgi', c_im.astype(f32), h_im))
    y = y.reshape(bsz, seq, D_SSM).astype(u.dtype) + d_skip * u
    y = jax.nn.gelu(y)
    return y * jax.nn.sigmoid(y @ glu_w + glu_b)


def _gmlp_branch(z_u, z_v, ln_g, ln_b, w_s, b_s):
    bsz, seq, _ = z_v.shape
    n_chunks = seq // CHUNK
    u = jax.nn.gelu(z_u)
    v = _layer_norm(jax.nn.gelu(z_v), ln_g, ln_b)
    vh = v.reshape(bsz, n_chunks, CHUNK, GMLP_HEADS, GMLP_HEAD_DIM)
    causal = jnp.tril(jnp.ones((CHUNK, CHUNK), dtype=bool))
    ws = jnp.where(causal[None], w_s, 0.0)
    s = jnp.einsum('hts,bcshd->bcthd', ws, vh) + b_s.T[:, :, None]
    return u * s.reshape(bsz, seq, D_GMLP)


def setup_inputs(seed: int = 0) -> dict:
    key = jax.random.key(seed)
    ks = jax.random.split(key, 40)
    f32 = jnp.float32

    def nrm(k, shape, scale):
        return jax.random.normal(k, shape, f32) * scale

    def gain(k, shape):
        return 1.0 + 0.02 * jax.random.normal(k, shape, f32)

    def bias(k, shape):
        return 0.02 * jax.random.normal(k, shape, f32)

    L = DEPTH
    x = jax.random.normal(ks[0], (BATCH, SEQ, D_MODEL), f32)
    p = jax.random.normal(ks[1], (DEPTH, BATCH, SEQ, PLE_DIM), f32)

    ffn1_w_in = nrm(ks[2], (L, D_MODEL, 2 * D_FF), D_MODEL ** -0.5)
    ffn1_w_out = nrm(ks[3], (L, D_FF, D_MODEL), D_FF ** -0.5 * DEEPNORM_BETA)
    ln1_g = gain(ks[4], (L, D_MODEL))
    ln1_b = bias(ks[5], (L, D_MODEL))

    mix_w_in = nrm(ks[6], (L, D_MODEL, D_IN), D_MODEL ** -0.5)
    n_idx = jnp.arange(SSM_STATE, dtype=f32)
    ssm_lambda_re = -0.5 + 0.01 * jax.random.normal(ks[7], (L, SSM_GROUPS, SSM_STATE), f32)
    ssm_lambda_im = (math.pi * n_idx)[None, None, :] + 0.01 * jax.random.normal(
        ks[8], (L, SSM_GROUPS, SSM_STATE), f32)
    ssm_log_dt = jax.random.uniform(ks[9], (L, SSM_GROUPS), f32,
                                    minval=math.log(1e-3), maxval=math.log(1e-1))
    b_scale = (2.0 * SSM_GROUP_CH) ** -0.5
    c_scale = (2.0 * SSM_STATE) ** -0.5
    ssm_b_re = nrm(ks[10], (L, SSM_GROUPS, SSM_STATE, SSM_GROUP_CH), b_scale)
    ssm_b_im = nrm(ks[11], (L, SSM_GROUPS, SSM_STATE, SSM_GROUP_CH), b_scale)
    ssm_c_re = nrm(ks[12], (L, SSM_GROUPS, SSM_GROUP_CH, SSM_STATE), c_scale)
    ssm_c_im = nrm(ks[13], (L, SSM_GROUPS, SSM_GROUP_CH, SSM_STATE), c_scale)
    ssm_d = nrm(ks[14], (L, D_SSM), 1.0)
    ssm_glu_w = nrm(ks[15], (L, D_SSM, D_SSM), D_SSM ** -0.5)
    ssm_glu_b = bias(ks[16], (L, D_SSM))

    gmlp_ln_g = gain(ks[17], (L, D_GMLP))
    gmlp_ln_b = bias(ks[18], (L, D_GMLP))
    gmlp_w_s = nrm(ks[19], (L, GMLP_HEADS, CHUNK, CHUNK), CHUNK ** -0.5)
    gmlp_b_s = 1.0 + 0.02 * jax.random.normal(ks[20], (L, GMLP_HEADS, CHUNK), f32)

    up_a = nrm(ks[21], (L, D_SSM, D_MODEL), D_SSM ** -0.5)
    up_b = nrm(ks[22], (L, D_GMLP, D_MODEL), D_GMLP ** -0.5)
    mix_w_out = nrm(ks[23], (L, D_MODEL, D_MODEL), D_MODEL ** -0.5 * DEEPNORM_BETA)
    ln2_g = gain(ks[24], (L, D_MODEL))
    ln2_b = bias(ks[25], (L, D_MODEL))

    ffn2_w_in = nrm(ks[26], (L, D_MODEL, 2 * D_FF), D_MODEL ** -0.5)
    ffn2_w_out = nrm(ks[27], (L, D_FF, D_MODEL), D_FF ** -0.5 * DEEPNORM_BETA)
    ln3_g = gain(ks[28], (L, D_MODEL))
    ln3_b = bias(ks[29], (L, D_MODEL))

    ple_w_proj = nrm(ks[30], (L, PLE_DIM, D_MODEL), PLE_DIM ** -0.5 * DEEPNORM_BETA)
    ple_w_gate = nrm(ks[31], (L, D_MODEL, D_MODEL), D_MODEL ** -0.5)

    return {
        "x": x, "p": p,
        "ffn1_w_in": ffn1_w_in, "ffn1_w_out": ffn1_w_out, "ln1_g": ln1_g, "ln1_b": ln1_b,
        "mix_w_in": mix_w_in,
        "ssm_lambda_re": ssm_lambda_re, "ssm_lambda_im": ssm_lambda_im, "ssm_log_dt": ssm_log_dt,
        "ssm_b_re": ssm_b_re, "ssm_b_im": ssm_b_im, "ssm_c_re": ssm_c_re, "ssm_c_im": ssm_c_im,
        "ssm_d": ssm_d, "ssm_glu_w": ssm_glu_w, "ssm_glu_b": ssm_glu_b,
        "gmlp_ln_g": gmlp_ln_g, "gmlp_ln_b": gmlp_ln_b, "gmlp_w_s": gmlp_w_s, "gmlp_b_s": gmlp_b_s,
        "up_a": up_a, "up_b": up_b, "mix_w_out": mix_w_out, "ln2_g": ln2_g, "ln2_b": ln2_b,
        "ffn2_w_in": ffn2_w_in, "ffn2_w_out": ffn2_w_out, "ln3_g": ln3_g, "ln3_b": ln3_b,
        "ple_w_proj": ple_w_proj, "ple_w_gate": ple_w_gate,
    }


def reference(x, p, ffn1_w_in, ffn1_w_out, ln1_g, ln1_b, mix_w_in,
              ssm_lambda_re, ssm_lambda_im, ssm_log_dt, ssm_b_re, ssm_b_im, ssm_c_re, ssm_c_im,
              ssm_d, ssm_glu_w, ssm_glu_b, gmlp_ln_g, gmlp_ln_b, gmlp_w_s, gmlp_b_s,
              up_a, up_b, mix_w_out, ln2_g, ln2_b, ffn2_w_in, ffn2_w_out, ln3_g, ln3_b,
              ple_w_proj, ple_w_gate):
    splits = [D_SSM, D_SSM + D_GMLP, D_SSM + 2 * D_GMLP, D_SSM + 2 * D_GMLP + D_MODEL]
    for i in range(DEPTH):
        x = _layer_norm(DEEPNORM_ALPHA * x + 0.5 * _swiglu(x, ffn1_w_in[i], ffn1_w_out[i]),
                        ln1_g[i], ln1_b[i])
        proj = x @ mix_w_in[i]
        z_a, z_u, z_v, g_a, g_b = jnp.split(proj, splits, axis=-1)
        y_a = _s5_branch(z_a, ssm_lambda_re[i], ssm_lambda_im[i], ssm_log_dt[i],
                         ssm_b_re[i], ssm_b_im[i], ssm_c_re[i], ssm_c_im[i],
                         ssm_d[i], ssm_glu_w[i], ssm_glu_b[i]) @ up_a[i]
        y_b = _gmlp_branch(z_u, z_v, gmlp_ln_g[i], gmlp_ln_b[i],
                           gmlp_w_s[i], gmlp_b_s[i]) @ up_b[i]
        mixed = (jax.nn.sigmoid(g_a) * y_a + jax.nn.sigmoid(g_b) * y_b) @ mix_w_out[i]
        x = _layer_norm(DEEPNORM_ALPHA * x + mixed, ln2_g[i], ln2_b[i])
        x = _layer_norm(DEEPNORM_ALPHA * x + 0.5 * _swiglu(x, ffn2_w_in[i], ffn2_w_out[i]),
                        ln3_g[i], ln3_b[i])
        x = x + jax.nn.sigmoid(x @ ple_w_gate[i]) * (p[i] @ ple_w_proj[i])
    return x
```

```python
import math
import numpy as np
import ml_dtypes
import concourse.bass as bass
import concourse.mybir as mybir
from concourse.bass_utils import run_bass_kernel_spmd

F32 = mybir.dt.float32
BF16 = mybir.dt.bfloat16
AF = mybir.ActivationFunctionType
ALU = mybir.AluOpType

D = 1024
DFF = 2816
NFC = 22
DIN = 3584
NTOK = 4096
ALPHA = 2.0 ** 0.25
EPS = 1e-5
NSLOT = 4
NDSEM = 48


def _esize(dt):
    return 4 if dt == F32 else 2


class Sched:
    ENG = ("pe", "act", "dve", "pool", "sp")
    G = 256

    def __init__(self, nc):
        self.nc = nc
        self.h = dict(pe=nc.tensor, act=nc.scalar, dve=nc.vector, pool=nc.gpsimd, sp=nc.sync)
        self.ops = []
        self.slots = {}
        self.seen = {e: {} for e in self.ENG}
        self.dnext = {"sp": 0, "pool": NDSEM // 2}
        self.dval = [0] * NDSEM
        self.dlast = [None] * NDSEM

    def _slots(self, ap):
        name = ap.name
        es = _esize(ap.dtype)
        dims = list(ap.ap)
        pstep = dims[0][0]
        off = ap.offset
        if pstep > 0:
            off = off % pstep
        fd = dims[1:]
        ispsum = "PSUM" in str(ap.space).upper()
        g = 2048 if ispsum else self.G
        if not fd:
            lo = off * es
            return [(name, lo // g)]
        base = off
        nd = []
        for st, cnt in fd:
            if st < 0:
                base += st * (cnt - 1)
                st = -st
            nd.append((st, cnt))
        inner = nd[-1]
        outer = nd[:-1]
        nout = 1
        for st, cnt in outer:
            nout *= cnt
        res = set()
        if nout > 512:
            hi = base + sum(st * (cnt - 1) for st, cnt in nd)
            for s in range((base * es) // g, (hi * es + es - 1) // g + 1):
                res.add((name, s))
            return res
        idx = [0] * len(outer)
        while True:
            o = base + sum(i * st for i, (st, cnt) in zip(idx, outer))
            lo = o * es
            hi = (o + inner[0] * (inner[1] - 1)) * es + es - 1
            for s in range(lo // g, hi // g + 1):
                res.add((name, s))
            k = len(outer) - 1
            while k >= 0:
                idx[k] += 1
                if idx[k] < outer[k][1]:
                    break
                idx[k] = 0
                k -= 1
            if k < 0:
                break
        return res

    def op(self, eng, fn, reads=(), writes=(), dma=False):
        deps_e = {}
        deps_d = {}

        def add(tok):
            if tok is None:
                return
            if tok[0] == "e":
                if tok[1] == eng and eng == "pe":
                    return
                deps_e[tok[1]] = max(deps_e.get(tok[1], -1), tok[2])
            else:
                deps_d[tok[1]] = max(deps_d.get(tok[1], 0), tok[2])

        rs = set()
        for ap in reads:
            rs |= set(self._slots(ap))
        ws = set()
        for ap in writes:
            ws |= set(self._slots(ap))
        for s in rs:
            st = self.slots.get(s)
            if st is not None:
                add(st[0])
        for s in ws:
            st = self.slots.get(s)
            if st is not None:
                add(st[0])
                for t in st[1].values():
                    add(t)
                for t in st[2]:
                    add(t)
        idx = len(self.ops)
        rec = dict(eng=eng, fn=fn, waits=[], inc=False, dsem=None)
        if dma:
            k = self.dnext[eng]
            base = 0 if eng == "sp" else NDSEM // 2
            self.dnext[eng] = base + (k - base + 1) % (NDSEM // 2)
            if self.dlast[k] is not None:
                add(("d", k, self.dlast[k]))
            self.dval[k] += 16
            self.dlast[k] = self.dval[k]
            rec["dsem"] = k
            tok = ("d", k, self.dval[k])
        else:
            tok = ("e", eng, idx)
        seen = self.seen[eng]
        for f, i in deps_e.items():
            if seen.get(("e", f), -1) >= i:
                continue
            seen[("e", f)] = i
            rec["waits"].append(("e", f, i))
        for k, v in deps_d.items():
            if seen.get(("d", k), 0) >= v:
                continue
            seen[("d", k)] = v
            rec["waits"].append(("d", k, v))
        self.ops.append(rec)
        for s in rs:
            st = self.slots.get(s)
            if st is None:
                st = [None, {}, []]
                self.slots[s] = st
            if tok[0] == "e":
                st[1][eng] = tok
            else:
                st[2].append(tok)
        for s in ws:
            self.slots[s] = [tok, {}, []]
        return idx

    def emit(self):
        nc = self.nc
        for rec in self.ops:
            for w in rec["waits"]:
                if w[0] == "e":
                    self.ops[w[2]]["inc"] = True
        esem = {e: nc.alloc_semaphore(name="es_" + e) for e in self.ENG}
        dsem = [nc.alloc_semaphore(name="ds_%d" % i) for i in range(NDSEM)]
        cnt = {e: 0 for e in self.ENG}
        val = {}
        for i, rec in enumerate(self.ops):
            if rec["inc"]:
                cnt[rec["eng"]] += 1
                val[i] = cnt[rec["eng"]]
        for i, rec in enumerate(self.ops):
            h = self.h[rec["eng"]]
            for w in rec["waits"]:
                if w[0] == "e":
                    h.wait_ge(esem[w[1]], val[w[2]])
                else:
                    h.wait_ge(dsem[w[1]], w[2])
            ins = rec["fn"]()
            if rec["dsem"] is not None:
                ins.then_inc(dsem[rec["dsem"]], 16)
            elif rec["inc"]:
                ins.then_inc(esem[rec["eng"]], 1)
        for k in range(NDSEM):
            if self.dlast[k] is not None:
                nc.sync.wait_ge(dsem[k], self.dlast[k])


def build(nst=4, dbg=None):
    nc = bass.Bass("TRN2", target_bir_lowering=False)
    S = Sched(nc)
    ntok = nst * 1024

    def din(name, shape):
        return nc.dram_tensor(name, list(shape), F32, kind="ExternalInput").ap()

    x_d = din("x", [NTOK, D])
    p_d = din("p", [NTOK, 256])
    w1i = din("ffn1_w_in", [D, 2 * DFF]); w1o = din("ffn1_w_out", [DFF, D])
    w2i = din("ffn2_w_in", [D, 2 * DFF]); w2o = din("ffn2_w_out", [DFF, D])
    ln_g = [din("ln%d_g" % i, [D]) for i in (1, 2, 3)]
    ln_b = [din("ln%d_b" % i, [D]) for i in (1, 2, 3)]
    wmi = din("mix_w_in", [D, DIN])
    lam_re = din("ssm_lambda_re", [32, 64]); lam_im = din("ssm_lambda_im", [32, 64])
    log_dt = din("ssm_log_dt", [32])
    b_re = din("ssm_b_re", [32, 64, 16]); b_im = din("ssm_b_im", [32, 64, 16])
    c_re = din("ssm_c_re", [32, 16, 64]); c_im = din("ssm_c_im", [32, 16, 64])
    ssm_d = din("ssm_d", [512])
    glu_w = din("ssm_glu_w", [512, 512]); glu_b = din("ssm_glu_b", [512])
    gm_g = din("gmlp_ln_g", [512]); gm_b = din("gmlp_ln_b", [512])
    gm_ws = din("gmlp_w_s", [8, 128, 128]); gm_bs = din("gmlp_b_s", [8, 128])
    up_a = din("up_a", [512, D]); up_b = din("up_b", [512, D])
    wmo = din("mix_w_out", [D, D])
    wpp = din("ple_w_proj", [256, D]); wpg = din("ple_w_gate", [D, D])
    c_identf = din("c_identf", [128, 128])
    c_mask = din("c_mask", [128, 128])
    c_hsel = din("c_hsel", [40, 512])
    out_d = nc.dram_tensor("out", [NTOK, D], F32, kind="ExternalOutput").ap()
    dbg_out = {}

    def sb(name, shape, dt):
        return nc.alloc_sbuf_tensor(name, list(shape), dt)

    X = sb("X", [128, 8, 1024], F32)
    xT = sb("xT", [128, 8, 1024], BF16)
    Xb = sb("Xb", [128, 1024], BF16)
    ring = sb("ring", [128, NSLOT, 4096], BF16)
    lnp = sb("lnp", [128, 2, 1024], F32)
    MI = sb("MI", [128, 32, 2, 64], BF16)
    MT = sb("MT", [128, 32, 128], BF16)
    MO = sb("MO", [64, 2, 32, 128], BF16)
    WS = sb("WS", [128, 8, 128], BF16)
    gbc = sb("gbc", [128, 2, 512], F32)
    hsel = sb("hsel", [40, 512], BF16)
    bsT = sb("bsT", [40, 128], BF16)
    identf = sb("identf", [128, 128], F32)
    identb = sb("identb", [128, 128], BF16)
    maskf = sb("maskf", [128, 128], F32)
    ARt = sb("ARt", [64, 64], F32)
    AIt = sb("AIt", [64, 64], F32)
    MB = 4
    AIJ = sb("AIJ", [64, MB, 64], F32)
    carry = sb("carry", [64, 64], F32)
    T1s = sb("T1s", [64, 64], F32)
    rt2 = sb("rt2", [64, 16, 64], F32)
    T2s = sb("T2s", [64, 64], F32)
    dcol = sb("dcol", [128, 32], F32)
    glub = sb("glub", [128, 4], F32)
    stats = sb("stats", [128, 8, 12], F32)
    mv = sb("mv", [128, 8, 2], F32)
    rstd = sb("rstd", [128, 8, 2], F32)
    neghalf = sb("neghalf", [128, 1], F32)
    AREN = 62464
    arena = sb("arena", [128, AREN // 2], BF16)
    psum = nc.alloc_psum_tensor("psum", [128, 8, 512], F32)

    def av(lo, nbytes, dt, shape=None):
        es = _esize(dt)
        assert lo % 4 == 0 and lo + nbytes <= AREN, (lo, nbytes)
        v = arena[:, lo // 2:(lo + nbytes) // 2]
        if dt == F32:
            v = v.bitcast(F32)
        return v

    gT = av(0, 22528, BF16).rearrange("p (f t) -> p f t", f=11)
    WO = av(22528, 22528, BF16).rearrange("p (f d) -> p f d", f=11)
    sgt = av(45056, 4096, F32).rearrange("p (r n) -> p r n", r=2)
    Ub = av(0, 8192, BF16).rearrange("p (g s j) -> p g s j", g=32, s=8)
    yg = av(20480, 8192, BF16).rearrange("p (s c) -> p s c", s=8)
    m2 = av(0, 8192, BF16).rearrange("p (c t) -> p c t", c=8)
    UT = av(8192, 4096, BF16).rearrange("p (g b) -> p g b", g=32)
    XS = av(12288, 16384, F32).rearrange("p (b c g) -> p b c g", b=64, c=2)
    ygT = av(12288, 4096, BF16).rearrange("p (c t) -> p c t", c=4)
    yain = av(16384, 4096, BF16).rearrange("p (c t) -> p c t", c=4)
    mtmp = av(20480, 8192, F32).rearrange("p (r n) -> p r n", r=4)
    Hb = av(28672, 8448, BF16).rearrange("p (c g s) -> p c g s", c=2, g=32)
    uT = av(37888, 4096, BF16).rearrange("p (c t) -> p c t", c=4)
    zv = av(41984, 4096, F32).rearrange("p (r n) -> p r n", r=2)
    ABv = sb("ABv", [128, 4, 512], BF16)
    ybpre = av(50176, 4096, BF16).rearrange("p (c t) -> p c t", c=4)
    mT = av(54272, 8192, BF16).rearrange("p (c t) -> p c t", c=8)

    pbank = [0]

    def bank():
        b = pbank[0]
        pbank[0] = (b + 1) % 8
        return psum[:, b, :]

    def mm(out, lhsT, rhs, start, stop):
        S.op("pe", lambda: nc.tensor.matmul(out, lhsT, rhs, start=start, stop=stop),
             reads=[lhsT, rhs], writes=[out])

    def tr(out, in_, ident):
        S.op("pe", lambda: nc.tensor.transpose(out, in_, ident), reads=[in_, ident], writes=[out])

    def act(out, in_, func, bias=None, scale=None):
        rd = [in_]
        kw = {}
        if bias is not None:
            kw["bias"] = bias
            if not isinstance(bias, (int, float)):
                rd.append(bias)
        if scale is not None:
            kw["scale"] = scale
            if not isinstance(scale, (int, float)):
                rd.append(scale)
        S.op("act", lambda: nc.scalar.activation(out, in_, func, **kw), reads=rd, writes=[out])

    def acopy(out, in_):
        act(out, in_, AF.Copy)

    def vcopy(out, in_):
        S.op("dve", lambda: nc.vector.tensor_copy(out, in_), reads=[in_], writes=[out])

    def tt(out, in0, in1, op):
        S.op("dve", lambda: nc.vector.tensor_tensor(out, in0, in1, op), reads=[in0, in1], writes=[out])

    def ts(out, in0, s1, s2, op0, op1=None):
        rd = [in0] + [s for s in (s1, s2) if s is not None and not isinstance(s, (int, float))]
        if op1 is None:
            S.op("dve", lambda: nc.vector.tensor_scalar(out, in0, s1, None, op0), reads=rd, writes=[out])
        else:
            S.op("dve", lambda: nc.vector.tensor_scalar(out, in0, s1, s2, op0, op1), reads=rd, writes=[out])

    def stt(out, in0, sc, in1, op0, op1):
        rd = [in0, in1] + ([] if isinstance(sc, (int, float)) else [sc])
        S.op("dve", lambda: nc.vector.scalar_tensor_tensor(out, in0, sc, in1, op0, op1), reads=rd, writes=[out])

    def ptt(out, in0, in1, op):
        S.op("pool", lambda: nc.gpsimd.tensor_tensor(out, in0, in1, op), reads=[in0, in1], writes=[out])

    def vmemset(ap, v):
        S.op("dve", lambda: nc.vector.memset(ap, v), writes=[ap])

    def dma(q, out, in_, slow=False):
        sbuf_r = [in_] if str(in_.space).upper().startswith("SB") else []
        sbuf_w = [out] if str(out.space).upper().startswith("SB") else []
        h = S.h[q]
        if slow:
            S.op(q, lambda: h.dma_start(out=out, in_=in_, allow_slow_non_contiguous=True),
                 reads=sbuf_r, writes=sbuf_w, dma=True)
        else:
            S.op(q, lambda: h.dma_start(out=out, in_=in_), reads=sbuf_r, writes=sbuf_w, dma=True)

    free_slots = list(range(NSLOT))

    def slot():
        assert free_slots, "ring exhausted"
        s = free_slots.pop(0)
        return ring[:, s, :]

    W = dict(mode="rec", reqs=[], idx=0, issued=0, slots={})

    def pump():
        while free_slots and W["issued"] < len(W["reqs"]):
            sl = slot()
            parts = W["reqs"][W["issued"]]
            W["slots"][W["issued"]] = sl
            W["issued"] += 1
            for vf, src in parts:
                dma("pool", vf(sl), src)

    def wload(parts):
        if W["mode"] == "rec":
            W["reqs"].append(parts)
            return slot()
        i = W["idx"]
        W["idx"] += 1
        if W["issued"] <= i:
            pump()
        assert W["issued"] > i, "ring deadlock"
        return W["slots"].pop(i)

    def sfree(v):
        pstep = v.ap[0][0]
        s = (v.offset % pstep) // 4096
        assert s not in free_slots
        free_slots.append(s)
        if W["mode"] == "play":
            pump()

    def v8(sl):
        return sl.rearrange("p (k n) -> p k n", k=8)

    def v4(sl):
        return sl.rearrange("p (k n) -> p k n", k=4)

    def wview(w, c0, nc_):
        return w.rearrange("(kc k) n -> k kc n", k=128)[:, :, c0:c0 + nc_]

    dma("sp", identf[:], c_identf[:, :])
    dma("sp", maskf[:], c_mask[:, :])
    dma("pool", hsel[:], c_hsel[:, :])
    dma("pool", identb[:], c_identf[:, :])
    dma("sp", gbc[:, 0, :], gm_g.partition_broadcast(128))
    dma("sp", gbc[:, 1, :], gm_b.partition_broadcast(128))
    dma("sp", glub[:], glu_b.rearrange("(c p) -> p c", p=128), slow=True)
    for s_ in range(8):
        dma("sp", dcol[s_ * 16:(s_ + 1) * 16, :], ssm_d.rearrange("(g j) -> j g", j=16), slow=True)
    for r in range(4):
        vmemset(ABv[:, r, :], 0.0)
    vmemset(neghalf[:], -0.5)

    wsl = av(0, 4096, F32).rearrange("p (h s) -> p h s", h=8)
    dma("sp", wsl, gm_ws.rearrange("h t s -> t h s"))
    for hh in range(8):
        if hh % 4 == 0:
            bk = bank()
        tr(bk[:, (hh % 4) * 128:(hh % 4 + 1) * 128], wsl[:, hh, :], identf[:])
        if hh % 4 == 3:
            tt(WS[:, hh - 3:hh + 1, :], bk.rearrange("p (h t) -> p h t", h=4),
               maskf[:].unsqueeze(1).broadcast_to([128, 4, 128]), ALU.mult)
    bsf = av(4096, 512, F32)
    bsh = av(4608, 512, F32)
    vmemset(bsT[:], 0.0)
    dma("sp", bsf[0:8, :], gm_bs[:, :])
    dma("sp", bsf[32:40, :], gm_bs[:, :])
    vcopy(bsT[0:8, :], bsf[0:8, :])
    bsh2 = av(5120, 256, BF16)
    vcopy(bsh2[32:40, :], bsf[32:40, :])
    vcopy(bsh[32:40, :], bsh2[32:40, :])
    tt(bsT[32:40, :], bsf[32:40, :], bsh[32:40, :], ALU.subtract)

    def s5_setup():
        sm = {}

        bump = {"early": 0, "late": 57344}

        def t64(name, n):
            if name == "QW":
                sm[name] = sb("s5_" + name, [64, n], F32)[:]
                return sm[name]
            pool_ = "late" if name in ("PW", "Wc") else "early"
            lo = bump[pool_]
            bump[pool_] = lo + ((n * 4 + 255) // 256) * 256
            assert bump[pool_] <= (16384 if pool_ == "early" else AREN)
            sm[name] = av(lo, n * 4, F32)[0:64, :]
            return sm[name]
        lt = t64("lt", 128)[0:32, :]
        dma("sp", lt[:, 0:64], lam_re[:, :])
        dma("sp", lt[:, 64:128], lam_im[:, :])
        LR = t64("LR", 32); LI = t64("LI", 32)
        bk = bank()
        tr(bk[0:64, 0:32], lt[:, 0:64], identf[0:32, 0:32])
        tr(bk[0:64, 32:64], lt[:, 64:128], identf[0:32, 0:32])
        vcopy(LR[:], bk[0:64, 0:32])
        vcopy(LI[:], bk[0:64, 32:64])
        DT = t64("DT", 32)
        dma("sp", DT[:], log_dt.partition_broadcast(64))
        act(DT[:], DT[:], AF.Exp)
        aa = t64("aa", 32); th = t64("th", 32); mag = t64("mag", 32)
        tt(aa[:], LR[:], DT[:], ALU.mult)
        tt(th[:], LI[:], DT[:], ALU.mult)
        act(mag[:], aa[:], AF.Exp)
        sa = t64("sa", 32); ca = t64("ca", 32); tb_w = t64("tb_w", 32)
        ts(ca[:], th[:], 0.5 * math.pi, None, ALU.add)
        vcopy(sa[:], th[:])
        wtmp = t64("wtmp", 32)
        for ang in (sa, ca):
            for j in range(8):
                ts(wtmp[:], ang[:], (2 * j + 1) * math.pi, 2 * math.pi, ALU.is_gt, ALU.mult)
                if j == 0:
                    tt(tb_w[:], ang[:], wtmp[:], ALU.subtract)
                else:
                    tt(tb_w[:], tb_w[:], wtmp[:], ALU.subtract)
            vcopy(ang[:], tb_w[:])
        act(sa[:], sa[:], AF.Sin)
        act(ca[:], ca[:], AF.Sin)
        PW = t64("PW", 9 * 64).rearrange("p (k c g) -> p k c g", k=9, c=2)
        vmemset(PW[:, 0, 0, :], 1.0)
        vmemset(PW[:, 0, 1, :], 0.0)
        tt(PW[:, 1, 0, :], mag[:], ca[:], ALU.mult)
        tt(PW[:, 1, 1, :], mag[:], sa[:], ALU.mult)
        ta = t64("ta", 32); tb = t64("tb", 32)
        for k in range(2, 9):
            tt(ta[:], PW[:, k - 1, 0, :], PW[:, 1, 0, :], ALU.mult)
            tt(tb[:], PW[:, k - 1, 1, :], PW[:, 1, 1, :], ALU.mult)
            tt(PW[:, k, 0, :], ta[:], tb[:], ALU.subtract)
            tt(ta[:], PW[:, k - 1, 0, :], PW[:, 1, 1, :], ALU.mult)
            tt(tb[:], PW[:, k - 1, 1, :], PW[:, 1, 0, :], ALU.mult)
            tt(PW[:, k, 1, :], ta[:], tb[:], ALU.add)
        vcopy(ARt[:, 0:32], PW[:, 8, 0, :])
        vcopy(ARt[:, 32:64], PW[:, 8, 0, :])
        ts(AIt[:, 0:32], PW[:, 8, 1, :], -1.0, None, ALU.mult)
        vcopy(AIt[:, 32:64], PW[:, 8, 1, :])
        QW = t64("QW", (MB + 1) * 64).rearrange("p (k c g) -> p k c g", k=MB + 1, c=2)
        vcopy(QW[:, 1, 0, :], PW[:, 8, 0, :])
        vcopy(QW[:, 1, 1, :], PW[:, 8, 1, :])
        for k in range(2, MB + 1):
            tt(ta[:], QW[:, k - 1, 0, :], QW[:, 1, 0, :], ALU.mult)
            tt(tb[:], QW[:, k - 1, 1, :], QW[:, 1, 1, :], ALU.mult)
            tt(QW[:, k, 0, :], ta[:], tb[:], ALU.subtract)
            tt(ta[:], QW[:, k - 1, 0, :], QW[:, 1, 1, :], ALU.mult)
            tt(tb[:], QW[:, k - 1, 1, :], QW[:, 1, 0, :], ALU.mult)
            tt(QW[:, k, 1, :], ta[:], tb[:], ALU.add)
        for k in range(1, MB + 1):
            ts(AIJ[:, k - 1, 0:32], QW[:, k, 1, :], -1.0, None, ALU.mult)
            vcopy(AIJ[:, k - 1, 32:64], QW[:, k, 1, :])
        sm["QWv"] = QW
        nr = t64("nr", 32); den = t64("den", 32); cr = t64("cr", 32); ci = t64("ci", 32)
        ts(nr[:], PW[:, 1, 0, :], 1.0, None, ALU.subtract)
        tt(den[:], LR[:], LR[:], ALU.mult)
        tt(ta[:], LI[:], LI[:], ALU.mult)
        tt(den[:], den[:], ta[:], ALU.add)
        S.op("dve", lambda: nc.vector.reciprocal(den[:], den[:]), reads=[den[:]], writes=[den[:]])
        tt(ta[:], nr[:], LR[:], ALU.mult)
        tt(tb[:], PW[:, 1, 1, :], LI[:], ALU.mult)
        tt(cr[:], ta[:], tb[:], ALU.add)
        tt(cr[:], cr[:], den[:], ALU.mult)
        tt(ta[:], PW[:, 1, 1, :], LR[:], ALU.mult)
        tt(tb[:], nr[:], LI[:], ALU.mult)
        tt(ci[:], ta[:], tb[:], ALU.subtract)
        tt(ci[:], ci[:], den[:], ALU.mult)
        Wc = t64("Wc", 8 * 64).rearrange("p (k c g) -> p k c g", k=8, c=2)
        tk = t64("tk", 8 * 32).rearrange("p (k g) -> p k g", k=8)
        tk2 = t64("tk2", 8 * 32).rearrange("p (k g) -> p k g", k=8)
        crb = cr[:].unsqueeze(1).broadcast_to([64, 8, 32])
        cib = ci[:].unsqueeze(1).broadcast_to([64, 8, 32])
        tt(tk, PW[:, 0:8, 0, :], crb, ALU.mult)
        tt(tk2, PW[:, 0:8, 1, :], cib, ALU.mult)
        tt(Wc[:, :, 0, :], tk, tk2, ALU.subtract)
        tt(tk, PW[:, 0:8, 0, :], cib, ALU.mult)
        tt(tk2, PW[:, 0:8, 1, :], crb, ALU.mult)
        tt(Wc[:, :, 1, :], tk, tk2, ALU.add)
        BR = av(45056, 2048, F32).rearrange("p (g j) -> p g j", g=32)
        BI = av(47104, 2048, F32).rearrange("p (g j) -> p g j", g=32)
        CR = av(49152, 2048, F32).rearrange("p (g i) -> p g i", g=32)
        CI = av(51200, 2048, F32).rearrange("p (g i) -> p g i", g=32)
        dma("sp", BR[0:64], b_re.rearrange("g p j -> p g j"))
        dma("sp", BI[0:64], b_im.rearrange("g p j -> p g j"))
        ct = av(53248, 2048, F32).rearrange("p (c q) -> p c q", c=4)
        dma("sp", ct[:, :, 0:64], c_re.rearrange("(c gl) i p -> (gl i) c p", c=4))
        dma("sp", ct[:, :, 64:128], c_im.rearrange("(c gl) i p -> (gl i) c p", c=4))
        bk1 = bank(); bk2 = bank()
        for c4 in range(4):
            tr(bk1[0:64, c4 * 128:(c4 + 1) * 128], ct[:, c4, 0:64], identf[:])
            tr(bk2[0:64, c4 * 128:(c4 + 1) * 128], ct[:, c4, 64:128], identf[:])
        vcopy(CR[0:64], bk1[0:64, :].rearrange("p (g i) -> p g i", g=32))
        vcopy(CI[0:64], bk2[0:64, :].rearrange("p (g i) -> p g i", g=32))
        tmpa = av(55296, 1024, F32).rearrange("p (g j) -> p g j", g=16)
        tmpb = av(56320, 1024, F32).rearrange("p (g j) -> p g j", g=16)
        for gh in range(2):
            g0 = 16 * gh
            MIT = av(0, 16384, F32).rearrange("p (c g s j) -> p c g s j", c=2, g=16, s=8)
            MOx = av(16384, 18432, F32).rearrange("p (c g k i) -> p c g k i", c=2, g=16, k=9)
            tmpA = av(34816, 8192, F32)[0:64, :]
            tmpB = av(43008, 2048, F32)[0:64, :]
            for half_s in range(2):
                s0 = 4 * half_s
                def wv(c):
                    return Wc[:, 7 - s0 - 3:7 - s0 + 1, c, g0:g0 + 16][:, ::-1, :].rearrange("p k g -> p g k") \
                        .unsqueeze(3).broadcast_to([64, 16, 4, 16])
                def bv(Bt):
                    return Bt[0:64, g0:g0 + 16, :].unsqueeze(2).broadcast_to([64, 16, 4, 16])
                tA = tmpA[:, 0:1024].rearrange("p (g s j) -> p g s j", g=16, s=4)
                tB = tmpA[:, 1024:2048].rearrange("p (g s j) -> p g s j", g=16, s=4)
                tt(tA, wv(0), bv(BR), ALU.mult)
                tt(tB, wv(1), bv(BI), ALU.mult)
                tt(MIT[0:64, 0, :, s0:s0 + 4, :], tA, tB, ALU.subtract)
                tt(tA, wv(0), bv(BI), ALU.mult)
                tt(tB, wv(1), bv(BR), ALU.mult)
                tt(MIT[0:64, 1, :, s0:s0 + 4, :], tA, tB, ALU.add)
            for k3 in range(3):
                k0 = 3 * k3
                def pv(c):
                    return PW[:, k0:k0 + 3, c, g0:g0 + 16].rearrange("p k g -> p g k") \
                        .unsqueeze(3).broadcast_to([64, 16, 3, 16])
                def cv(Ct):
                    return Ct[0:64, g0:g0 + 16, :].unsqueeze(2).broadcast_to([64, 16, 3, 16])
                tA = tmpA[:, 0:768].rearrange("p (g s j) -> p g s j", g=16, s=3)
                tB = tmpA[:, 1024:1792].rearrange("p (g s j) -> p g s j", g=16, s=3)
                tt(tA, cv(CR), pv(0), ALU.mult)
                tt(tB, cv(CI), pv(1), ALU.mult)
                tt(MOx[0:64, 0, :, k0:k0 + 3, :], tA, tB, ALU.subtract)
                tt(tA, cv(CR), pv(1), ALU.mult)
                tt(tB, cv(CI), pv(0), ALU.mult)
                stt(MOx[0:64, 1, :, k0:k0 + 3, :], tA, -1.0, tB, ALU.mult, ALU.subtract)
            for c in range(2):
                acopy(MO[:, c, g0:g0 + 16, :].rearrange("p g (k i) -> p g k i", k=8),
                      MOx[0:64, c, :, 1:9, :])
            for gl in range(16):
                if gl % 4 == 0:
                    bk = bank()
                for c in range(2):
                    col = ((gl % 4) * 2 + c) * 64
                    tr(bk[:, col:col + 64], MIT[0:64, c, gl, :, :].rearrange("p s j -> p (s j)"),
                       identf[0:64, 0:64])
                if gl % 4 == 3:
                    acopy(MI[:, g0 + gl - 3:g0 + gl + 1, :, :],
                          bk.rearrange("p (g c q) -> p g c q", g=4, c=2))
            KKs = av(34816, 8192, F32).rearrange("p (g n) -> p g n", g=16)
            for gl in range(16):
                if gl % 4 == 0:
                    bk = bank()
                o = bk[0:16, (gl % 4) * 128:(gl % 4 + 1) * 128]
                for c in range(2):
                    mm(o, MIT[0:64, c, gl, 7, :], MOx[0:64, c, gl, 0:8, :].rearrange("p k i -> p (k i)"),
                       start=(c == 0), stop=(c == 1))
                if gl % 4 == 3:
                    vcopy(KKs[0:16, gl - 3:gl + 1, :], bk[0:16, :].rearrange("p (g n) -> p g n", g=4))
            for gl in range(16):
                g = g0 + gl
                stt(KKs[0:16, gl, 0:16], identf[0:16, 0:16], dcol[0:16, g:g + 1], KKs[0:16, gl, 0:16],
                    ALU.mult, ALU.add)
            if gh == 0:
                vmemset(MT[:].rearrange("p g n -> p (g n)"), 0.0)
            for s_ in range(8):
                dma("pool", MT[s_ * 16:(s_ + 1) * 16, g0:g0 + 16, s_ * 16:128], KKs[0:16, :, 0:(8 - s_) * 16])
        return sm


    def load_ln(i):
        dma("sp", lnp[:, 0, :], ln_g[i].partition_broadcast(128))
        dma("sp", lnp[:, 1, :], ln_b[i].partition_broadcast(128))

    lnrot = [0]

    class LNPipe:
        def __init__(self):
            self.q = []

        def add(self, t, eps):
            r = lnrot[0]
            lnrot[0] = (r + 1) % 8
            e = dict(t=t, r=r, age=1, s2a=False, bk=None)
            S.op("dve", lambda: nc.vector.bn_stats(stats[:, r, 0:6], X[:, t, 0:512]),
                 reads=[X[:, t, 0:512]], writes=[stats[:, r, 0:6]])
            S.op("dve", lambda: nc.vector.bn_stats(stats[:, r, 6:12], X[:, t, 512:1024]),
                 reads=[X[:, t, 512:1024]], writes=[stats[:, r, 6:12]])
            S.op("dve", lambda: nc.vector.bn_aggr(mv[:, r, :], stats[:, r, :]),
                 reads=[stats[:, r, :]], writes=[mv[:, r, :]])
            ts(rstd[:, r, 0:1], mv[:, r, 1:2], eps, None, ALU.add)
            ptt(rstd[:, r, 0:1], rstd[:, r, 0:1], neghalf[:], ALU.pow)
            self.q.append(e)

        def pre(self):
            for e in list(self.q):
                t, r = e["t"], e["r"]
                xt = X[:, t, :]
                if e["age"] == 3:
                    acopy(xT[:, :, t * 128:(t + 1) * 128], e["bk"].rearrange("p (k t) -> p k t", k=8))
                    self.q.remove(e)
                elif e["age"] == 2:
                    tt(xt, xt, lnp[:, 0, :], ALU.mult)
                    tt(xt, xt, lnp[:, 1, :], ALU.add)
                    acopy(Xb[:], xt)
                    e["s2a"] = True
                elif e["age"] == 1:
                    ts(rstd[:, r, 1:2], mv[:, r, 0:1], rstd[:, r, 0:1], -1.0, ALU.mult, ALU.mult)
                    act(xt, xt, AF.Identity, bias=rstd[:, r, 1:2], scale=rstd[:, r, 0:1])
                    e["age"] = 2

        def mid(self):
            for e in self.q:
                if e["age"] == 2 and e["s2a"]:
                    bk = bank().bitcast(BF16)
                    for kc in range(8):
                        tr(bk[:, kc * 128:(kc + 1) * 128], Xb[:, kc * 128:(kc + 1) * 128], identb[:])
                    e["bk"] = bk
                    e["age"] = 3

        def flush(self):
            while self.q:
                self.pre()
                self.mid()

        def flush_fns(self):
            for e in list(self.q):
                if e["age"] == 3:
                    t = e["t"]
                    acopy(xT[:, :, t * 128:(t + 1) * 128], e["bk"].rearrange("p (k t) -> p k t", k=8))
                    self.q.remove(e)
            return [self.flush]

    def to_xT(t):
        acopy(Xb[:], X[:, t, :])
        bk = bank().bitcast(BF16)
        for kc in range(8):
            tr(bk[:, kc * 128:(kc + 1) * 128], Xb[:, kc * 128:(kc + 1) * 128], identb[:])
        vcopy(xT[:, :, t * 128:(t + 1) * 128], bk.rearrange("p (k t) -> p k t", k=8))

    def ffn(w_in, w_out, ln_i, hook=None):
        load_ln(ln_i)
        for hf in range(2):
            pipe = LNPipe()
            for pr in range(6):
                fcs = [f for f in (hf * 11 + 2 * pr, hf * 11 + 2 * pr + 1) if f < hf * 11 + 11]
                nf = len(fcs)
                c0 = fcs[0] * 128
                sl = wload([
                    ((lambda sl, nf=nf: sl[:, 0:2048].rearrange("p (k n) -> p k n", k=8)[:, :, 0:nf * 128]),
                     wview(w_in, c0, nf * 128)),
                    ((lambda sl, nf=nf: sl[:, 2048:4096].rearrange("p (k n) -> p k n", k=8)[:, :, 0:nf * 128]),
                     wview(w_in, DFF + c0, nf * 128))])
                gv = sl[:, 0:2048].rearrange("p (k n) -> p k n", k=8)
                uv = sl[:, 2048:4096].rearrange("p (k n) -> p k n", k=8)
                if pr == 2:
                    wov = w_out.rearrange("(f k) d -> k f d", k=128)
                    for q in range(0, 11, 4):
                        n = min(4, 11 - q)
                        dma("pool", WO[:, q:q + n, :], wov[:, hf * 11 + q:hf * 11 + q + n, :])
                for j, fc in enumerate(fcs):
                    fl = fc - hf * 11
                    for thf in range(2):
                        bg = bank(); bu = bank()
                        for kc in range(8):
                            mm(bg, gv[:, kc, j * 128:(j + 1) * 128], xT[:, kc, thf * 512:(thf + 1) * 512],
                               start=(kc == 0), stop=(kc == 7))
                        for kc in range(8):
                            mm(bu, uv[:, kc, j * 128:(j + 1) * 128], xT[:, kc, thf * 512:(thf + 1) * 512],
                               start=(kc == 0), stop=(kc == 7))
                        rr = (fl * 2 + thf) % 2
                        act(sgt[:, rr, :], bg, AF.Silu)
                        tt(gT[:, fl, thf * 512:(thf + 1) * 512], bu, sgt[:, rr, :], ALU.mult)
                sfree(sl)
            if hf == 1 and hook is not None:
                hook()
            for t in range(8):
                if hf == 1:
                    pipe.pre()
                bks = [bank(), bank()]
                for dh in range(2):
                    for fl in range(11):
                        mm(bks[dh], gT[:, fl, t * 128:(t + 1) * 128], WO[:, fl, dh * 512:(dh + 1) * 512],
                           start=(fl == 0), stop=(fl == 10))
                if hf == 1:
                    pipe.mid()
                for dh in range(2):
                    xs = X[:, t, dh * 512:(dh + 1) * 512]
                    if hf == 0:
                        stt(xs, xs, 2.0 * ALPHA, bks[dh], ALU.mult, ALU.add)
                    else:
                        tt(xs, xs, bks[dh], ALU.add)
                if hf == 1:
                    pipe.add(t, 4.0 * EPS)
            pipe.flush()

    def mix_half(h, first, deferred=()):
        T0 = 4 * h
        c0 = h * 512
        s_za = v8(wload([(v8, wview(wmi, 0, 512))]))
        s_zu = v8(wload([(v8, wview(wmi, 512, 512))]))
        s_zv = v8(wload([(v8, wview(wmi, 1024, 512))]))
        for s_ in range(8):
            bk = bank()
            for kc in range(8):
                mm(bk[0:64, :], xT[:, kc, c0 + s_:c0 + 512:8], s_za[:, kc, :], start=(kc == 0), stop=(kc == 7))
            acopy(Ub[0:64, :, s_, :], bk[0:64, :].rearrange("p (g j) -> p g j", g=32))
        sfree(s_za)
        for g2 in range(2):
            bk = bank().bitcast(BF16)
            for gl in range(16):
                g = g2 * 16 + gl
                tr(bk[:, gl * 64:(gl + 1) * 64], Ub[0:64, g, :, :].rearrange("p s j -> p (s j)"),
                   identb[0:64, 0:64])
            acopy(UT[:, g2 * 16:(g2 + 1) * 16, :], bk.rearrange("p (g b) -> p g b", g=16))
        for q in range(8):
            bk = bank()
            for gl in range(4):
                g = q * 4 + gl
                for c in range(2):
                    col = (gl * 2 + c) * 64
                    mm(bk[0:64, col:col + 64], MI[:, g, c, :], UT[:, g, :], start=True, stop=True)
            bv = bk[0:64, :].rearrange("p (g c b) -> p g c b", g=4, c=2)
            for c in range(2):
                acopy(XS[0:64, :, c, q * 4:q * 4 + 4].rearrange("p b g -> p g b"), bv[:, :, c, :])
        for cc in range(4):
            bk = bank()
            for kc in range(8):
                mm(bk, s_zu[:, kc, cc * 128:(cc + 1) * 128], xT[:, kc, c0:c0 + 512], start=(kc == 0), stop=(kc == 7))
            act(uT[:, cc, :], bk, AF.Gelu)
        sfree(s_zu)
        for q in range(2):
            s_ga = v8(wload([(v8, wview(wmi, 1536 + q * 512, 512))]))
            for dl in range(4):
                dc = q * 4 + dl
                bk = bank()
                for kc in range(8):
                    mm(bk, s_ga[:, kc, dl * 128:(dl + 1) * 128], xT[:, kc, c0:c0 + 512], start=(kc == 0), stop=(kc == 7))
                act(mT[:, dc, :], bk, AF.Sigmoid)
            sfree(s_ga)
        QW = SM["QWv"]
        if first:
            vmemset(carry[:], 0.0)
        KS = 64 // MB
        XSv = XS[0:64]
        rt1 = av(46080, 4096, F32).rearrange("p (k n) -> p k n", k=16)

        def arv(j, K):
            return QW[:, j, 0, :].unsqueeze(1).unsqueeze(1).broadcast_to([64, K, 2, 32])

        def aiv(j, K):
            return AIJ[:, j - 1, :].rearrange("p (c g) -> p c g", c=2).unsqueeze(1).broadcast_to([64, K, 2, 32])

        def cmul_add(dst, src, j, K):
            t1 = rt1[0:64, 0:K, :].rearrange("p k (c g) -> p k c g", c=2)
            t2 = rt2[:, 0:K, :].rearrange("p k (c g) -> p k c g", c=2)
            tt(t1, arv(j, K), src, ALU.mult)
            tt(t2, aiv(j, K), src[:, :, ::-1, :], ALU.mult)
            tt(dst, dst, t1, ALU.add)
            tt(dst, dst, t2, ALU.add)

        def rec_prefix():
            for i in range(1, MB):
                for k0 in range(0, KS, 16):
                    K = min(16, KS - k0)
                    src = XSv[:, (k0 * MB + i - 1):(k0 + K) * MB:MB, :, :]
                    dst = XSv[:, (k0 * MB + i):(k0 + K) * MB:MB, :, :]
                    cmul_add(dst, src, 1, K)

        def rec_seq(k0, k1):
            ar = QW[:, MB, 0, :].unsqueeze(1).broadcast_to([64, 2, 32])
            ai = AIJ[:, MB - 1, :].rearrange("p (c g) -> p c g", c=2)
            t1 = T1s[:].rearrange("p (c g) -> p c g", c=2)
            t2 = T2s[:].rearrange("p (c g) -> p c g", c=2)
            for k in range(k0, k1):
                prev = carry[:].rearrange("p (c g) -> p c g", c=2) if k == 0 else XSv[:, k * MB - 1, :, :]
                cur_ = XSv[:, k * MB + MB - 1, :, :]
                tt(t1, ar, prev, ALU.mult)
                tt(t2, ai, prev[:, ::-1, :], ALU.mult)
                tt(t1, t1, cur_, ALU.add)
                tt(cur_, t1, t2, ALU.add)

        def rec_fill():
            for i in range(MB - 1):
                j = i + 1
                cmul_add(XSv[:, i:i + 1, :, :], carry[:].rearrange("p (k c g) -> p k c g", k=1, c=2), j, 1)
                for k0 in range(1, KS, 16):
                    K = min(16, KS - k0)
                    src = XSv[:, (k0 * MB - 1):(k0 + K) * MB - 1:MB, :, :]
                    dst = XSv[:, (k0 * MB + i):(k0 + K) * MB:MB, :, :]
                    cmul_add(dst, src, j, K)
                for c in range(2):
                    acopy(Hb[0:64, c, :, 1 + i:65:MB], XSv[:, i::MB, c, :].rearrange("p b g -> p g b"))
            acopy(Hb[0:64, :, :, 0], carry[:].rearrange("p (c g) -> p c g", c=2))
            vcopy(carry[:].rearrange("p (c g) -> p c g", c=2), XSv[:, 63, :, :])

        def rec_seq_b():
            rec_seq(KS // 2, KS)
            for c in range(2):
                acopy(Hb[0:64, c, :, MB:65:MB], XSv[:, MB - 1::MB, c, :].rearrange("p b g -> p g b"))

        rec_plan = [rec_prefix, (lambda: rec_seq(0, KS // 2)), rec_seq_b, rec_fill]

        def rec_steps(b0, b1):
            rec_plan[b0 // 16]()

        for fn_ in deferred:
            fn_()
        def gmlp_tile(tl):
            t = T0 + tl
            bk = bank()
            for kc in range(8):
                mm(bk, xT[:, kc, t * 128:(t + 1) * 128], s_zv[:, kc, :], start=(kc == 0), stop=(kc == 7))
            r = tl % 2
            act(zv[:, r, :], bk, AF.Gelu)
            lr = lnrot[0]
            lnrot[0] = (lr + 1) % 8
            S.op("dve", (lambda r=r, lr=lr: nc.vector.bn_stats(stats[:, lr, 0:6], zv[:, r, :])),
                 reads=[zv[:, r, :]], writes=[stats[:, lr, 0:6]])
            S.op("dve", (lambda lr=lr: nc.vector.bn_aggr(mv[:, lr, :], stats[:, lr, 0:6])),
                 reads=[stats[:, lr, 0:6]], writes=[mv[:, lr, :]])
            ts(rstd[:, lr, 0:1], mv[:, lr, 1:2], EPS, None, ALU.add)
            ptt(rstd[:, lr, 0:1], rstd[:, lr, 0:1], neghalf[:], ALU.pow)
            ts(zv[:, r, :], zv[:, r, :], mv[:, lr, 0:1], rstd[:, lr, 0:1], ALU.subtract, ALU.mult)
            tt(zv[:, r, :], zv[:, r, :], gbc[:, 0, :], ALU.mult)
            z4 = zv[:, r, :].rearrange("p (q e d) -> p q e d", q=4, e=2)
            b4 = gbc[:, 1, :].rearrange("p (q e d) -> p q e d", q=4, e=2)
            A_ = ABv[:, r, :].rearrange("p (q e d) -> p q e d", q=4, e=2)
            B_ = ABv[:, 2 + r, :].rearrange("p (q e d) -> p q e d", q=4, e=2)
            tt(A_[:, :, 0, :], z4[:, :, 0, :], b4[:, :, 0, :], ALU.add)
            tt(B_[:, :, 1, :], z4[:, :, 1, :], b4[:, :, 1, :], ALU.add)
            bk = bank()
            for cc in range(4):
                o = bk[:, cc * 128:(cc + 1) * 128]
                mm(o, ABv[:, r, cc * 128:(cc + 1) * 128], WS[:, 2 * cc, :], start=True, stop=False)
                mm(o, ABv[:, 2 + r, cc * 128:(cc + 1) * 128], WS[:, 2 * cc + 1, :], start=False, stop=False)
                mm(o, hsel[0:40, cc * 128:(cc + 1) * 128], bsT[0:40, :], start=False, stop=True)
            tt(ybpre[:, :, tl * 128:(tl + 1) * 128], bk.rearrange("p (c t) -> p c t", c=4),
               uT[:, :, tl * 128:(tl + 1) * 128], ALU.mult)

        s_upb = v4(wload([(v4, up_b.rearrange("(kc k) n -> k kc n", k=128))]))

        def yb_part(dcs):
            for dc in dcs:
                if dc % 4 == 0:
                    s_gb[0] = v8(wload([(v8, wview(wmi, 2560 + dc * 128, 512))]))
                b_yb = bank(); b_gb = bank()
                for kc in range(4):
                    mm(b_yb, s_upb[:, kc, dc * 128:(dc + 1) * 128], ybpre[:, kc, :], start=(kc == 0), stop=(kc == 3))
                o = (dc % 4) * 128
                for kc in range(8):
                    mm(b_gb, s_gb[0][:, kc, o:o + 128], xT[:, kc, c0:c0 + 512], start=(kc == 0), stop=(kc == 7))
                act(zv[:, dc % 2, :], b_gb, AF.Sigmoid)
                tt(m2[:, dc, :], b_yb, zv[:, dc % 2, :], ALU.mult)
                if dc % 4 == 3:
                    sfree(s_gb[0])

        s_gb = [None]

        def rec_stream():
            for b0 in (0, 16, 32, 48):
                rec_steps(b0, b0 + 16)

        def side_stream():
            for tl in range(4):
                gmlp_tile(tl)
            sfree(s_zv)
            yb_part(range(0, 8))
            sfree(s_upb)

        rec_steps(0, 16)
        gmlp_tile(0)
        gmlp_tile(1)
        rec_steps(16, 32)
        gmlp_tile(2)
        gmlp_tile(3)
        sfree(s_zv)
        rec_steps(32, 48)
        yb_part(range(0, 4))
        rec_steps(48, 64)
        yb_part(range(4, 8))
        sfree(s_upb)
        for q in range(8):
            bk = bank()
            for gl in range(4):
                g = q * 4 + gl
                o = bk[0:64, gl * 128:(gl + 1) * 128]
                mm(o, UT[:, g, :], MT[:, g, :], start=True, stop=False)
                mm(o, Hb[0:64, 0, g, 0:64], MO[:, 0, g, :], start=False, stop=False)
                mm(o, Hb[0:64, 1, g, 0:64], MO[:, 1, g, :], start=False, stop=True)
            act(yg[0:64, :, q * 64:(q + 1) * 64].rearrange("p s (g i) -> p s g i", g=4),
                bk[0:64, :].rearrange("p (g s i) -> p s g i", g=4, s=8), AF.Gelu)
        for c2 in range(2):
            bk = bank().bitcast(BF16)
            for ci_ in range(2):
                cc = c2 * 2 + ci_
                for s_ in range(8):
                    tr(bk[:, ci_ * 512 + s_ * 64:ci_ * 512 + s_ * 64 + 64], yg[0:64, s_, cc * 128:(cc + 1) * 128],
                       identb[0:64, 0:64])
            acopy(ygT[:, c2 * 2:c2 * 2 + 2, :].rearrange("p c (b s) -> p c s b", s=8),
                  bk.rearrange("p (c s b) -> p c s b", c=2, s=8))
        vglu = (lambda sl: sl[:, 0:2048].rearrange("p (k n) -> p k n", k=4))
        s_glu = vglu(wload([(vglu, glu_w.rearrange("(kc k) n -> k kc n", k=128))]))
        s_upa = v4(wload([(v4, up_a.rearrange("(kc k) n -> k kc n", k=128))]))
        for cc in range(4):
            bk = bank()
            for kc in range(4):
                mm(bk, s_glu[:, kc, cc * 128:(cc + 1) * 128], ygT[:, kc, :], start=(kc == 0), stop=(kc == 3))
            r = cc % 4
            act(mtmp[:, r, :], bk, AF.Sigmoid, bias=glub[:, cc:cc + 1])
            tt(yain[:, cc, :], ygT[:, cc, :], mtmp[:, r, :], ALU.mult)
        sfree(s_glu)
        for dc in range(8):
            b_ya = bank()
            for kc in range(4):
                mm(b_ya, s_upa[:, kc, dc * 128:(dc + 1) * 128], yain[:, kc, :], start=(kc == 0), stop=(kc == 3))
            r = dc % 4
            tt(mtmp[:, r, :], b_ya, mT[:, dc, :], ALU.mult)
            tt(mT[:, dc, :], mtmp[:, r, :], m2[:, dc, :], ALU.add)
        sfree(s_upa)
        s_wo = []
        pipe2 = LNPipe()
        for dh in range(2):
            s_wo.append(v8(wload([(v8, wview(wmo, dh * 512, 512))])))
        for tl in range(4):
            t = T0 + tl
            pipe2.pre()
            bks = [bank(), bank()]
            for dh in range(2):
                for kc in range(8):
                    mm(bks[dh], mT[:, kc, tl * 128:(tl + 1) * 128], s_wo[dh][:, kc, :], start=(kc == 0), stop=(kc == 7))
            pipe2.mid()
            for dh in range(2):
                xs = X[:, t, dh * 512:(dh + 1) * 512]
                stt(xs, xs, ALPHA, bks[dh], ALU.mult, ALU.add)
            pipe2.add(t, EPS)
        sfree(s_wo[0])
        sfree(s_wo[1])
        return pipe2.flush_fns()

    ptall = av(49152, 8192, F32).rearrange("p (t n) -> p t n", t=8)
    pTall = av(57344, 4096, BF16).rearrange("p (t k n) -> p t k n", t=8, k=2)
    pbt2 = av(61440, 1024, BF16).rearrange("p (r n) -> p r n", r=2)
    sgp4 = av(0, 8192, F32).rearrange("p (r n) -> p r n", r=4)

    def ple_prep(st):
        for t in range(8):
            r0 = st * 1024 + t * 128
            dma("sp", ptall[:, t, :], p_d[r0:r0 + 128, :])
        for t4 in range(2):
            bk = bank().bitcast(BF16)
            for tl in range(4):
                t = t4 * 4 + tl
                acopy(pbt2[:, t % 2, :], ptall[:, t, :])
                for kc in range(2):
                    tr(bk[:, (tl * 2 + kc) * 128:(tl * 2 + kc + 1) * 128], pbt2[:, t % 2, kc * 128:(kc + 1) * 128],
                       identb[:])
            vcopy(pTall[:, t4 * 4:t4 * 4 + 4, :, :], bk.rearrange("p (t k n) -> p t k n", t=4, k=2))

    def ple(st):
        s_pg = []
        for dh in range(2):
            s_pg.append(v8(wload([(v8, wview(wpg, dh * 512, 512))])))
        vpp = (lambda sl: sl[:, 0:2048].rearrange("p (k n) -> p k n", k=2))
        s_pp = vpp(wload([(vpp, wpp.rearrange("(kc k) n -> k kc n", k=128))]))
        for t in range(8):
            r0 = st * 1024 + t * 128
            for dh in range(2):
                bg = bank(); bp = bank()
                for kc in range(8):
                    mm(bg, xT[:, kc, t * 128:(t + 1) * 128], s_pg[dh][:, kc, :], start=(kc == 0), stop=(kc == 7))
                for kc in range(2):
                    mm(bp, pTall[:, t, kc, :], s_pp[:, kc, dh * 512:(dh + 1) * 512], start=(kc == 0), stop=(kc == 1))
                r = (t * 2 + dh) % 4
                act(sgp4[:, r, :], bg, AF.Sigmoid)
                tt(sgp4[:, r, :], bp, sgp4[:, r, :], ALU.mult)
                xs = X[:, t, dh * 512:(dh + 1) * 512]
                tt(xs, xs, sgp4[:, r, :], ALU.add)
            dma("sp", out_d[r0:r0 + 128, :], X[:, t, :])
        sfree(s_pg[0]); sfree(s_pg[1]); sfree(s_pp)

    for t in range(8):
        dma("sp", X[:, t, :], x_d[t * 128:(t + 1) * 128, :])
    for t in range(8):
        to_xT(t)
    SM = s5_setup()

    class _Stop(Exception):
        pass

    def stage(name):
        if dbg and dbg.get("stop") == name:
            raise _Stop()

    def emit_all():
      try:
          stage("setup")
          for st in range(nst):
              if st > 0:
                  for t in range(8):
                      r0 = st * 1024 + t * 128
                      dma("sp", X[:, t, :], x_d[r0:r0 + 128, :])
                  for t in range(8):
                      to_xT(t)
              stage("xT%d" % st)
              ffn(w1i, w1o, 0)
              stage("ffn1_%d" % st)
              load_ln(1)
              dfr = []
              for h in range(2):
                  dfr = mix_half(h, first=(st % 2 == 0 and h == 0), deferred=dfr)
                  stage("mix%d_%d" % (h, st))
              for fn_ in dfr:
                  fn_()
              ffn(w2i, w2o, 2, hook=(lambda st=st: ple_prep(st)))
              stage("ffn2_%d" % st)
              ple(st)
      except _Stop:
        pass


    class _Dummy:
        def op(self, *a, **k):
            return None

    class _Cap:
        def __init__(self, h):
            self.h = h
            self.rec = []

        def op(self, eng, fn, reads=(), writes=(), dma=False):
            self.rec.append((eng, fn, tuple(reads), tuple(writes), dma))

    def capture(f):
        nonlocal S
        base = S
        cap = _Cap(base.h)
        S = cap
        try:
            f()
        finally:
            S = base
        return cap.rec

    def merge_streams(a, b, na, nb):
        ia = ib = 0
        while ia < len(a) or ib < len(b):
            for (lst, n_, which) in ((a, na, 0), (b, nb, 1)):
                cnt = 0
                while True:
                    i = ia if which == 0 else ib
                    if i >= len(lst) or cnt >= n_:
                        break
                    o = lst[i]
                    if which == 0:
                        ia += 1
                    else:
                        ib += 1
                    S.op(o[0], o[1], reads=o[2], writes=o[3], dma=o[4])
                    cnt += (o[0] == "dve")

    S_real = S
    saved = (pbank[0], lnrot[0], list(free_slots))
    S = _Dummy()
    S.h = S_real.h
    emit_all()
    pbank[0], lnrot[0] = saved[0], saved[1]
    free_slots[:] = saved[2]
    S = S_real
    W["mode"] = "play"
    pump()
    emit_all()

    if dbg:
        lv = dict(X=X[:], xT=xT[:], MI=MI[:], MT=MT[:], MO=MO[:], WS=WS[:], bsT=bsT[:], ARt=ARt[:], AIt=AIt[:],
                  gT=gT, UT=UT[:, :, :], XS=XS[0:64], Hb=Hb[0:64], yg=yg[0:64], ygT=ygT, yain=yain, uT=uT,
                  ybpre=ybpre, mT=mT, Ub=Ub[0:64], dcol=dcol[:], glub=glub[:])
        for k_, v_ in SM.items():
            lv['s5_' + k_] = v_[:]
        for name in dbg.get("dump", []):
            ap = lv[name]
            dt_ = nc.dram_tensor("dbg_" + name, list(ap.shape), ap.dtype, kind="ExternalOutput").ap()
            dma("sp", dt_, ap)
    print('sbuf bytes remaining', nc.sbuf_bytes_remaining)
    S.emit()
    return nc


_NC = {}


def _consts():
    identf = np.eye(128, dtype=np.float32)
    mask = np.triu(np.ones((128, 128), dtype=np.float32))
    hsel = np.zeros((40, 512), dtype=np.float32)
    for h in range(8):
        hsel[h, h * 64:(h + 1) * 64] = 1.0
        hsel[32 + h, h * 64:(h + 1) * 64] = 1.0
    return dict(c_identf=identf, c_mask=mask, c_hsel=hsel)


def kernel(**inputs):
    n = 8
    if "nc" not in _NC:
        _NC["nc"] = build(4)
    nc = _NC["nc"]
    x = np.ascontiguousarray(np.asarray(inputs["x"], dtype=np.float32))
    p = np.ascontiguousarray(np.asarray(inputs["p"], dtype=np.float32))
    shared = {}
    for k, v in inputs.items():
        if k in ("x", "p"):
            continue
        a = np.asarray(v, dtype=np.float32)
        shared[k] = np.ascontiguousarray(a[0])
    shared.update(_consts())
    in_maps = []
    for c in range(n):
        m = dict(shared)
        m["x"] = x[2 * c:2 * c + 2].reshape(NTOK, D)
        m["p"] = p[0, 2 * c:2 * c + 2].reshape(NTOK, 256)
        in_maps.append(m)
    res = run_bass_kernel_spmd(nc, in_maps, core_ids=list(range(n)))
    out = np.stack([np.asarray(r["out"], dtype=np.float32).reshape(2, 2048, D) for r in res.results], axis=0)
    return out.reshape(16, 2048, D)
```

```python
import math
import numpy as np
import ml_dtypes
import concourse.bass as bass
import concourse.mybir as mybir
from concourse.bass_utils import run_bass_kernel_spmd

F32 = mybir.dt.float32
BF16 = mybir.dt.bfloat16
AF = mybir.ActivationFunctionType
ALU = mybir.AluOpType

D = 1024
DFF = 2816
NFC = 22
DIN = 3584
NTOK = 4096
ALPHA = 2.0 ** 0.25
EPS = 1e-5
NSLOT = 4
NDSEM = 48


def _esize(dt):
    return 4 if dt == F32 else 2


class Sched:
    ENG = ("pe", "act", "dve", "pool", "sp")
    G = 256

    def __init__(self, nc):
        self.nc = nc
        self.h = dict(pe=nc.tensor, act=nc.scalar, dve=nc.vector, pool=nc.gpsimd, sp=nc.sync)
        self.ops = []
        self.slots = {}
        self.seen = {e: {} for e in self.ENG}
        self.dnext = {"sp": 0, "pool": NDSEM // 2}
        self.dval = [0] * NDSEM
        self.dlast = [None] * NDSEM

    def _slots(self, ap):
        name = ap.name
        es = _esize(ap.dtype)
        dims = list(ap.ap)
        pstep = dims[0][0]
        off = ap.offset
        if pstep > 0:
            off = off % pstep
        fd = dims[1:]
        ispsum = "PSUM" in str(ap.space).upper()
        g = 2048 if ispsum else self.G
        if not fd:
            lo = off * es
            return [(name, lo // g)]
        base = off
        nd = []
        for st, cnt in fd:
            if st < 0:
                base += st * (cnt - 1)
                st = -st
            nd.append((st, cnt))
        inner = nd[-1]
        outer = nd[:-1]
        nout = 1
        for st, cnt in outer:
            nout *= cnt
        res = set()
        if nout > 512:
            hi = base + sum(st * (cnt - 1) for st, cnt in nd)
            for s in range((base * es) // g, (hi * es + es - 1) // g + 1):
                res.add((name, s))
            return res
        idx = [0] * len(outer)
        while True:
            o = base + sum(i * st for i, (st, cnt) in zip(idx, outer))
            lo = o * es
            hi = (o + inner[0] * (inner[1] - 1)) * es + es - 1
            for s in range(lo // g, hi // g + 1):
                res.add((name, s))
            k = len(outer) - 1
            while k >= 0:
                idx[k] += 1
                if idx[k] < outer[k][1]:
                    break
                idx[k] = 0
                k -= 1
            if k < 0:
                break
        return res

    def op(self, eng, fn, reads=(), writes=(), dma=False):
        deps_e = {}
        deps_d = {}

        def add(tok):
            if tok is None:
                return
            if tok[0] == "e":
                if tok[1] == eng and eng == "pe":
                    return
                deps_e[tok[1]] = max(deps_e.get(tok[1], -1), tok[2])
            else:
                deps_d[tok[1]] = max(deps_d.get(tok[1], 0), tok[2])

        rs = set()
        for ap in reads:
            rs |= set(self._slots(ap))
        ws = set()
        for ap in writes:
            ws |= set(self._slots(ap))
        for s in rs:
            st = self.slots.get(s)
            if st is not None:
                add(st[0])
        for s in ws:
            st = self.slots.get(s)
            if st is not None:
                add(st[0])
                for t in st[1].values():
                    add(t)
                for t in st[2]:
                    add(t)
        idx = len(self.ops)
        rec = dict(eng=eng, fn=fn, waits=[], inc=False, dsem=None)
        if dma:
            k = self.dnext[eng]
            base = 0 if eng == "sp" else NDSEM // 2
            self.dnext[eng] = base + (k - base + 1) % (NDSEM // 2)
            if self.dlast[k] is not None:
                add(("d", k, self.dlast[k]))
            self.dval[k] += 16
            self.dlast[k] = self.dval[k]
            rec["dsem"] = k
            tok = ("d", k, self.dval[k])
        else:
            tok = ("e", eng, idx)
        seen = self.seen[eng]
        for f, i in deps_e.items():
            if seen.get(("e", f), -1) >= i:
                continue
            seen[("e", f)] = i
            rec["waits"].append(("e", f, i))
        for k, v in deps_d.items():
            if seen.get(("d", k), 0) >= v:
                continue
            seen[("d", k)] = v
            rec["waits"].append(("d", k, v))
        self.ops.append(rec)
        for s in rs:
            st = self.slots.get(s)
            if st is None:
                st = [None, {}, []]
                self.slots[s] = st
            if tok[0] == "e":
                st[1][eng] = tok
            else:
                st[2].append(tok)
        for s in ws:
            self.slots[s] = [tok, {}, []]
        return idx

    def emit(self):
        nc = self.nc
        for rec in self.ops:
            for w in rec["waits"]:
                if w[0] == "e":
                    self.ops[w[2]]["inc"] = True
        esem = {e: nc.alloc_semaphore(name="es_" + e) for e in self.ENG}
        dsem = [nc.alloc_semaphore(name="ds_%d" % i) for i in range(NDSEM)]
        cnt = {e: 0 for e in self.ENG}
        val = {}
        for i, rec in enumerate(self.ops):
            if rec["inc"]:
                cnt[rec["eng"]] += 1
                val[i] = cnt[rec["eng"]]
        for i, rec in enumerate(self.ops):
            h = self.h[rec["eng"]]
            for w in rec["waits"]:
                if w[0] == "e":
                    h.wait_ge(esem[w[1]], val[w[2]])
                else:
                    h.wait_ge(dsem[w[1]], w[2])
            ins = rec["fn"]()
            if rec["dsem"] is not None:
                ins.then_inc(dsem[rec["dsem"]], 16)
            elif rec["inc"]:
                ins.then_inc(esem[rec["eng"]], 1)
        for k in range(NDSEM):
            if self.dlast[k] is not None:
                nc.sync.wait_ge(dsem[k], self.dlast[k])


def build(nst=4, dbg=None):
    nc = bass.Bass("TRN2", target_bir_lowering=False)
    S = Sched(nc)
    ntok = nst * 1024

    def din(name, shape):
        return nc.dram_tensor(name, list(shape), F32, kind="ExternalInput").ap()

    x_d = din("x", [NTOK, D])
    p_d = din("p", [NTOK, 256])
    w1i = din("ffn1_w_in", [D, 2 * DFF]); w1o = din("ffn1_w_out", [DFF, D])
    w2i = din("ffn2_w_in", [D, 2 * DFF]); w2o = din("ffn2_w_out", [DFF, D])
    ln_g = [din("ln1_g", [D]), din("ln2_g", [D]), din("ln3_g", [D])]
    ln_b = [din("ln1_b", [D]), din("ln2_b", [D]), din("ln3_b", [D])]
    wmi = din("mix_w_in", [D, DIN])
    lam_re = din("ssm_lambda_re", [32, 64]); lam_im = din("ssm_lambda_im", [32, 64])
    log_dt = din("ssm_log_dt", [32])
    b_re = din("ssm_b_re", [32, 64, 16]); b_im = din("ssm_b_im", [32, 64, 16])
    c_re = din("ssm_c_re", [32, 16, 64]); c_im = din("ssm_c_im", [32, 16, 64])
    ssm_d = din("ssm_d", [512])
    glu_w = din("ssm_glu_w", [512, 512]); glu_b = din("ssm_glu_b", [512])
    gm_g = din("gmlp_ln_g", [512]); gm_b = din("gmlp_ln_b", [512])
    gm_ws = din("gmlp_w_s", [8, 128, 128]); gm_bs = din("gmlp_b_s", [8, 128])
    up_a = din("up_a", [512, D]); up_b = din("up_b", [512, D])
    wmo = din("mix_w_out", [D, D])
    wpp = din("ple_w_proj", [256, D]); wpg = din("ple_w_gate", [D, D])
    c_identf = din("c_identf", [128, 128])
    c_mask = din("c_mask", [128, 128])
    c_hsel = din("c_hsel", [40, 512])
    out_d = nc.dram_tensor("out", [NTOK, D], F32, kind="ExternalOutput").ap()
    dbg_out = {}

    def sb(name, shape, dt):
        return nc.alloc_sbuf_tensor(name, list(shape), dt)

    X = sb("X", [128, 8, 1024], F32)
    xT = sb("xT", [128, 8, 1024], BF16)
    Xb = sb("Xb", [128, 1024], BF16)
    ring = sb("ring", [128, NSLOT, 4096], BF16)
    lnp = sb("lnp", [128, 2, 1024], F32)
    MI = sb("MI", [128, 32, 2, 64], BF16)
    MT = sb("MT", [128, 32, 128], BF16)
    MO = sb("MO", [64, 2, 32, 128], BF16)
    WS = sb("WS", [128, 8, 128], BF16)
    gbc = sb("gbc", [128, 2, 512], F32)
    hsel = sb("hsel", [40, 512], BF16)
    bsT = sb("bsT", [40, 128], BF16)
    identf = sb("identf", [128, 128], F32)
    identb = sb("identb", [128, 128], BF16)
    maskf = sb("maskf", [128, 128], F32)
    ARt = sb("ARt", [64, 64], F32)
    AIt = sb("AIt", [64, 64], F32)
    MB = 4
    AIJ = sb("AIJ", [64, MB, 64], F32)
    carry = sb("carry", [64, 64], F32)
    T1s = sb("T1s", [64, 64], F32)
    rt2 = sb("rt2", [64, 16, 64], F32)
    T2s = sb("T2s", [64, 64], F32)
    dcol = sb("dcol", [128, 32], F32)
    glub = sb("glub", [128, 4], F32)
    stats = sb("stats", [128, 8, 12], F32)
    mv = sb("mv", [128, 8, 2], F32)
    rstd = sb("rstd", [128, 8, 2], F32)
    neghalf = sb("neghalf", [128, 1], F32)
    AREN = 62464
    arena = sb("arena", [128, AREN // 2], BF16)
    psum = nc.alloc_psum_tensor("psum", [128, 8, 512], F32)

    def av(lo, nbytes, dt, shape=None):
        es = _esize(dt)
        assert lo % 4 == 0 and lo + nbytes <= AREN, (lo, nbytes)
        v = arena[:, lo // 2:(lo + nbytes) // 2]
        if dt == F32:
            v = v.bitcast(F32)
        return v

    gT = av(0, 22528, BF16).rearrange("p (f t) -> p f t", f=11)
    WO = av(22528, 22528, BF16).rearrange("p (f d) -> p f d", f=11)
    sgt = av(45056, 4096, F32).rearrange("p (r n) -> p r n", r=2)
    Ub = av(0, 8192, BF16).rearrange("p (g s j) -> p g s j", g=32, s=8)
    yg = av(20480, 8192, BF16).rearrange("p (s c) -> p s c", s=8)
    m2 = av(0, 8192, BF16).rearrange("p (c t) -> p c t", c=8)
    UT = av(8192, 4096, BF16).rearrange("p (g b) -> p g b", g=32)
    XS = av(12288, 16384, F32).rearrange("p (b c g) -> p b c g", b=64, c=2)
    ygT = av(12288, 4096, BF16).rearrange("p (c t) -> p c t", c=4)
    yain = av(16384, 4096, BF16).rearrange("p (c t) -> p c t", c=4)
    mtmp = av(20480, 8192, F32).rearrange("p (r n) -> p r n", r=4)
    Hb = av(28672, 8448, BF16).rearrange("p (c g s) -> p c g s", c=2, g=32)
    uT = av(37888, 4096, BF16).rearrange("p (c t) -> p c t", c=4)
    zv = av(41984, 4096, F32).rearrange("p (r n) -> p r n", r=2)
    ABv = sb("ABv", [128, 4, 512], BF16)
    ybpre = av(50176, 4096, BF16).rearrange("p (c t) -> p c t", c=4)
    mT = av(54272, 8192, BF16).rearrange("p (c t) -> p c t", c=8)

    pbank = [0]

    def bank():
        b = pbank[0]
        pbank[0] = (b + 1) % 8
        return psum[:, b, :]

    def mm(out, lhsT, rhs, start, stop):
        S.op("pe", lambda: nc.tensor.matmul(out, lhsT, rhs, start=start, stop=stop),
             reads=[lhsT, rhs], writes=[out])

    def tr(out, in_, ident):
        S.op("pe", lambda: nc.tensor.transpose(out, in_, ident), reads=[in_, ident], writes=[out])

    def act(out, in_, func, bias=None, scale=None):
        rd = [in_]
        kw = {}
        if bias is not None:
            kw["bias"] = bias
            if not isinstance(bias, (int, float)):
                rd.append(bias)
        if scale is not None:
            kw["scale"] = scale
            if not isinstance(scale, (int, float)):
                rd.append(scale)
        S.op("act", lambda: nc.scalar.activation(out, in_, func, **kw), reads=rd, writes=[out])

    def acopy(out, in_):
        act(out, in_, AF.Copy)

    def vcopy(out, in_):
        S.op("dve", lambda: nc.vector.tensor_copy(out, in_), reads=[in_], writes=[out])

    def tt(out, in0, in1, op):
        S.op("dve", lambda: nc.vector.tensor_tensor(out, in0, in1, op), reads=[in0, in1], writes=[out])

    def ts(out, in0, s1, s2, op0, op1=None):
        rd = [in0] + [s for s in (s1, s2) if s is not None and not isinstance(s, (int, float))]
        if op1 is None:
            S.op("dve", lambda: nc.vector.tensor_scalar(out, in0, s1, None, op0), reads=rd, writes=[out])
        else:
            S.op("dve", lambda: nc.vector.tensor_scalar(out, in0, s1, s2, op0, op1), reads=rd, writes=[out])

    def stt(out, in0, sc, in1, op0, op1):
        rd = [in0, in1] + ([] if isinstance(sc, (int, float)) else [sc])
        S.op("dve", lambda: nc.vector.scalar_tensor_tensor(out, in0, sc, in1, op0, op1), reads=rd, writes=[out])

    def ptt(out, in0, in1, op):
        S.op("pool", lambda: nc.gpsimd.tensor_tensor(out, in0, in1, op), reads=[in0, in1], writes=[out])

    def vmemset(ap, v):
        S.op("dve", lambda: nc.vector.memset(ap, v), writes=[ap])

    def dma(q, out, in_, slow=False):
        sbuf_r = [in_] if str(in_.space).upper().startswith("SB") else []
        sbuf_w = [out] if str(out.space).upper().startswith("SB") else []
        h = S.h[q]
        if slow:
            S.op(q, lambda: h.dma_start(out=out, in_=in_, allow_slow_non_contiguous=True),
                 reads=sbuf_r, writes=sbuf_w, dma=True)
        else:
            S.op(q, lambda: h.dma_start(out=out, in_=in_), reads=sbuf_r, writes=sbuf_w, dma=True)

    free_slots = list(range(NSLOT))

    def slot():
        assert free_slots, "ring exhausted"
        s = free_slots.pop(0)
        return ring[:, s, :]

    W = dict(mode="rec", reqs=[], idx=0, issued=0, slots={})

    def pump():
        while free_slots and W["issued"] < len(W["reqs"]):
            sl = slot()
            parts = W["reqs"][W["issued"]]
            W["slots"][W["issued"]] = sl
            W["issued"] += 1
            for vf, src in parts:
                dma("pool", vf(sl), src)

    def wload(parts):
        if W["mode"] == "rec":
            W["reqs"].append(parts)
            return slot()
        i = W["idx"]
        W["idx"] += 1
        if W["issued"] <= i:
            pump()
        assert W["issued"] > i, "ring deadlock"
        return W["slots"].pop(i)

    def sfree(v):
        pstep = v.ap[0][0]
        s = (v.offset % pstep) // 4096
        assert s not in free_slots
        free_slots.append(s)
        if W["mode"] == "play":
            pump()

    def v8(sl):
        return sl.rearrange("p (k n) -> p k n", k=8)

    def v4(sl):
        return sl.rearrange("p (k n) -> p k n", k=4)

    def wview(w, c0, nc_):
        return w.rearrange("(kc k) n -> k kc n", k=128)[:, :, c0:c0 + nc_]

    dma("sp", identf[:], c_identf[:, :])
    dma("sp", maskf[:], c_mask[:, :])
    dma("pool", hsel[:], c_hsel[:, :])
    dma("pool", identb[:], c_identf[:, :])
    dma("sp", gbc[:, 0, :], gm_g.partition_broadcast(128))
    dma("sp", gbc[:, 1, :], gm_b.partition_broadcast(128))
    dma("sp", glub[:], glu_b.rearrange("(c p) -> p c", p=128), slow=True)
    for s_ in range(8):
        dma("sp", dcol[s_ * 16:(s_ + 1) * 16, :], ssm_d.rearrange("(g j) -> j g", j=16), slow=True)
    for r in range(4):
        vmemset(ABv[:, r, :], 0.0)
    vmemset(neghalf[:], -0.5)

    wsl = av(0, 4096, F32).rearrange("p (h s) -> p h s", h=8)
    dma("sp", wsl, gm_ws.rearrange("h t s -> t h s"))
    for hh in range(8):
        if hh % 4 == 0:
            bk = bank()
        tr(bk[:, (hh % 4) * 128:(hh % 4 + 1) * 128], wsl[:, hh, :], identf[:])
        if hh % 4 == 3:
            tt(WS[:, hh - 3:hh + 1, :], bk.rearrange("p (h t) -> p h t", h=4),
               maskf[:].unsqueeze(1).broadcast_to([128, 4, 128]), ALU.mult)
    bsf = av(4096, 512, F32)
    bsh = av(4608, 512, F32)
    vmemset(bsT[:], 0.0)
    dma("sp", bsf[0:8, :], gm_bs[:, :])
    dma("sp", bsf[32:40, :], gm_bs[:, :])
    vcopy(bsT[0:8, :], bsf[0:8, :])
    bsh2 = av(5120, 256, BF16)
    vcopy(bsh2[32:40, :], bsf[32:40, :])
    vcopy(bsh[32:40, :], bsh2[32:40, :])
    tt(bsT[32:40, :], bsf[32:40, :], bsh[32:40, :], ALU.subtract)

    def s5_setup():
        sm = {}

        bump = {"early": 0, "late": 57344}

        def t64(name, n):
            if name == "QW":
                sm[name] = sb("s5_" + name, [64, n], F32)[:]
                return sm[name]
            pool_ = "late" if name in ("PW", "Wc") else "early"
            lo = bump[pool_]
            bump[pool_] = lo + ((n * 4 + 255) // 256) * 256
            assert bump[pool_] <= (16384 if pool_ == "early" else AREN)
            sm[name] = av(lo, n * 4, F32)[0:64, :]
            return sm[name]
        lt = t64("lt", 128)[0:32, :]
        dma("sp", lt[:, 0:64], lam_re[:, :])
        dma("sp", lt[:, 64:128], lam_im[:, :])
        LR = t64("LR", 32); LI = t64("LI", 32)
        bk = bank()
        tr(bk[0:64, 0:32], lt[:, 0:64], identf[0:32, 0:32])
        tr(bk[0:64, 32:64], lt[:, 64:128], identf[0:32, 0:32])
        vcopy(LR[:], bk[0:64, 0:32])
        vcopy(LI[:], bk[0:64, 32:64])
        DT = t64("DT", 32)
        dma("sp", DT[:], log_dt.partition_broadcast(64))
        act(DT[:], DT[:], AF.Exp)
        aa = t64("aa", 32); th = t64("th", 32); mag = t64("mag", 32)
        tt(aa[:], LR[:], DT[:], ALU.mult)
        tt(th[:], LI[:], DT[:], ALU.mult)
        act(mag[:], aa[:], AF.Exp)
        sa = t64("sa", 32); ca = t64("ca", 32); tb_w = t64("tb_w", 32)
        ts(ca[:], th[:], 0.5 * math.pi, None, ALU.add)
        vcopy(sa[:], th[:])
        wtmp = t64("wtmp", 32)
        for ang in (sa, ca):
            for j in range(8):
                ts(wtmp[:], ang[:], (2 * j + 1) * math.pi, 2 * math.pi, ALU.is_gt, ALU.mult)
                if j == 0:
                    tt(tb_w[:], ang[:], wtmp[:], ALU.subtract)
                else:
                    tt(tb_w[:], tb_w[:], wtmp[:], ALU.subtract)
            vcopy(ang[:], tb_w[:])
        act(sa[:], sa[:], AF.Sin)
        act(ca[:], ca[:], AF.Sin)
        PW = t64("PW", 9 * 64).rearrange("p (k c g) -> p k c g", k=9, c=2)
        vmemset(PW[:, 0, 0, :], 1.0)
        vmemset(PW[:, 0, 1, :], 0.0)
        tt(PW[:, 1, 0, :], mag[:], ca[:], ALU.mult)
        tt(PW[:, 1, 1, :], mag[:], sa[:], ALU.mult)
        ta = t64("ta", 32); tb = t64("tb", 32)
        for k in range(2, 9):
            tt(ta[:], PW[:, k - 1, 0, :], PW[:, 1, 0, :], ALU.mult)
            tt(tb[:], PW[:, k - 1, 1, :], PW[:, 1, 1, :], ALU.mult)
            tt(PW[:, k, 0, :], ta[:], tb[:], ALU.subtract)
            tt(ta[:], PW[:, k - 1, 0, :], PW[:, 1, 1, :], ALU.mult)
            tt(tb[:], PW[:, k - 1, 1, :], PW[:, 1, 0, :], ALU.mult)
            tt(PW[:, k, 1, :], ta[:], tb[:], ALU.add)
        vcopy(ARt[:, 0:32], PW[:, 8, 0, :])
        vcopy(ARt[:, 32:64], PW[:, 8, 0, :])
        ts(AIt[:, 0:32], PW[:, 8, 1, :], -1.0, None, ALU.mult)
        vcopy(AIt[:, 32:64], PW[:, 8, 1, :])
        QW = t64("QW", (MB + 1) * 64).rearrange("p (k c g) -> p k c g", k=MB + 1, c=2)
        vcopy(QW[:, 1, 0, :], PW[:, 8, 0, :])
        vcopy(QW[:, 1, 1, :], PW[:, 8, 1, :])
        for k in range(2, MB + 1):
            tt(ta[:], QW[:, k - 1, 0, :], QW[:, 1, 0, :], ALU.mult)
            tt(tb[:], QW[:, k - 1, 1, :], QW[:, 1, 1, :], ALU.mult)
            tt(QW[:, k, 0, :], ta[:], tb[:], ALU.subtract)
            tt(ta[:], QW[:, k - 1, 0, :], QW[:, 1, 1, :], ALU.mult)
            tt(tb[:], QW[:, k - 1, 1, :], QW[:, 1, 0, :], ALU.mult)
            tt(QW[:, k, 1, :], ta[:], tb[:], ALU.add)
        for k in range(1, MB + 1):
            ts(AIJ[:, k - 1, 0:32], QW[:, k, 1, :], -1.0, None, ALU.mult)
            vcopy(AIJ[:, k - 1, 32:64], QW[:, k, 1, :])
        sm["QWv"] = QW
        nr = t64("nr", 32); den = t64("den", 32); cr = t64("cr", 32); ci = t64("ci", 32)
        ts(nr[:], PW[:, 1, 0, :], 1.0, None, ALU.subtract)
        tt(den[:], LR[:], LR[:], ALU.mult)
        tt(ta[:], LI[:], LI[:], ALU.mult)
        tt(den[:], den[:], ta[:], ALU.add)
        S.op("dve", lambda: nc.vector.reciprocal(den[:], den[:]), reads=[den[:]], writes=[den[:]])
        tt(ta[:], nr[:], LR[:], ALU.mult)
        tt(tb[:], PW[:, 1, 1, :], LI[:], ALU.mult)
        tt(cr[:], ta[:], tb[:], ALU.add)
        tt(cr[:], cr[:], den[:], ALU.mult)
        tt(ta[:], PW[:, 1, 1, :], LR[:], ALU.mult)
        tt(tb[:], nr[:], LI[:], ALU.mult)
        tt(ci[:], ta[:], tb[:], ALU.subtract)
        tt(ci[:], ci[:], den[:], ALU.mult)
        Wc = t64("Wc", 8 * 64).rearrange("p (k c g) -> p k c g", k=8, c=2)
        tk = t64("tk", 8 * 32).rearrange("p (k g) -> p k g", k=8)
        tk2 = t64("tk2", 8 * 32).rearrange("p (k g) -> p k g", k=8)
        crb = cr[:].unsqueeze(1).broadcast_to([64, 8, 32])
        cib = ci[:].unsqueeze(1).broadcast_to([64, 8, 32])
        tt(tk, PW[:, 0:8, 0, :], crb, ALU.mult)
        tt(tk2, PW[:, 0:8, 1, :], cib, ALU.mult)
        tt(Wc[:, :, 0, :], tk, tk2, ALU.subtract)
        tt(tk, PW[:, 0:8, 0, :], cib, ALU.mult)
        tt(tk2, PW[:, 0:8, 1, :], crb, ALU.mult)
        tt(Wc[:, :, 1, :], tk, tk2, ALU.add)
        BR = av(45056, 2048, F32).rearrange("p (g j) -> p g j", g=32)
        BI = av(47104, 2048, F32).rearrange("p (g j) -> p g j", g=32)
        CR = av(49152, 2048, F32).rearrange("p (g i) -> p g i", g=32)
        CI = av(51200, 2048, F32).rearrange("p (g i) -> p g i", g=32)
        dma("sp", BR[0:64], b_re.rearrange("g p j -> p g j"))
        dma("sp", BI[0:64], b_im.rearrange("g p j -> p g j"))
        ct = av(53248, 2048, F32).rearrange("p (c q) -> p c q", c=4)
        dma("sp", ct[:, :, 0:64], c_re.rearrange("(c gl) i p -> (gl i) c p", c=4))
        dma("sp", ct[:, :, 64:128], c_im.rearrange("(c gl) i p -> (gl i) c p", c=4))
        bk1 = bank(); bk2 = bank()
        for c4 in range(4):
            tr(bk1[0:64, c4 * 128:(c4 + 1) * 128], ct[:, c4, 0:64], identf[:])
            tr(bk2[0:64, c4 * 128:(c4 + 1) * 128], ct[:, c4, 64:128], identf[:])
        vcopy(CR[0:64], bk1[0:64, :].rearrange("p (g i) -> p g i", g=32))
        vcopy(CI[0:64], bk2[0:64, :].rearrange("p (g i) -> p g i", g=32))
        tmpa = av(55296, 1024, F32).rearrange("p (g j) -> p g j", g=16)
        tmpb = av(56320, 1024, F32).rearrange("p (g j) -> p g j", g=16)
        for gh in range(2):
            g0 = 16 * gh
            MIT = av(0, 16384, F32).rearrange("p (c g s j) -> p c g s j", c=2, g=16, s=8)
            MOx = av(16384, 18432, F32).rearrange("p (c g k i) -> p c g k i", c=2, g=16, k=9)
            for s_ in range(8):
                k = 7 - s_
                wr = Wc[:, k, 0, g0:g0 + 16].unsqueeze(2).broadcast_to([64, 16, 16])
                wi = Wc[:, k, 1, g0:g0 + 16].unsqueeze(2).broadcast_to([64, 16, 16])
                tt(tmpa[0:64], wr, BR[0:64, g0:g0 + 16, :], ALU.mult)
                tt(tmpb[0:64], wi, BI[0:64, g0:g0 + 16, :], ALU.mult)
                tt(MIT[0:64, 0, :, s_, :], tmpa[0:64], tmpb[0:64], ALU.subtract)
                tt(tmpa[0:64], wr, BI[0:64, g0:g0 + 16, :], ALU.mult)
                tt(tmpb[0:64], wi, BR[0:64, g0:g0 + 16, :], ALU.mult)
                tt(MIT[0:64, 1, :, s_, :], tmpa[0:64], tmpb[0:64], ALU.add)
            for k in range(9):
                pr = PW[:, k, 0, g0:g0 + 16].unsqueeze(2).broadcast_to([64, 16, 16])
                pi = PW[:, k, 1, g0:g0 + 16].unsqueeze(2).broadcast_to([64, 16, 16])
                tt(tmpa[0:64], CR[0:64, g0:g0 + 16, :], pr, ALU.mult)
                tt(tmpb[0:64], CI[0:64, g0:g0 + 16, :], pi, ALU.mult)
                tt(MOx[0:64, 0, :, k, :], tmpa[0:64], tmpb[0:64], ALU.subtract)
                tt(tmpa[0:64], CR[0:64, g0:g0 + 16, :], pi, ALU.mult)
                tt(tmpb[0:64], CI[0:64, g0:g0 + 16, :], pr, ALU.mult)
                stt(MOx[0:64, 1, :, k, :], tmpa[0:64], -1.0, tmpb[0:64], ALU.mult, ALU.subtract)
            for c in range(2):
                acopy(MO[:, c, g0:g0 + 16, :].rearrange("p g (k i) -> p g k i", k=8),
                      MOx[0:64, c, :, 1:9, :])
            for gl in range(16):
                if gl % 4 == 0:
                    bk = bank()
                for c in range(2):
                    col = ((gl % 4) * 2 + c) * 64
                    tr(bk[:, col:col + 64], MIT[0:64, c, gl, :, :].rearrange("p s j -> p (s j)"),
                       identf[0:64, 0:64])
                if gl % 4 == 3:
                    acopy(MI[:, g0 + gl - 3:g0 + gl + 1, :, :],
                          bk.rearrange("p (g c q) -> p g c q", g=4, c=2))
            KKs = av(34816, 8192, F32).rearrange("p (g n) -> p g n", g=16)
            for gl in range(16):
                if gl % 4 == 0:
                    bk = bank()
                o = bk[0:16, (gl % 4) * 128:(gl % 4 + 1) * 128]
                for c in range(2):
                    mm(o, MIT[0:64, c, gl, 7, :], MOx[0:64, c, gl, 0:8, :].rearrange("p k i -> p (k i)"),
                       start=(c == 0), stop=(c == 1))
                if gl % 4 == 3:
                    vcopy(KKs[0:16, gl - 3:gl + 1, :], bk[0:16, :].rearrange("p (g n) -> p g n", g=4))
            for gl in range(16):
                g = g0 + gl
                stt(KKs[0:16, gl, 0:16], identf[0:16, 0:16], dcol[0:16, g:g + 1], KKs[0:16, gl, 0:16],
                    ALU.mult, ALU.add)
            if gh == 0:
                vmemset(MT[:].rearrange("p g n -> p (g n)"), 0.0)
            for s_ in range(8):
                dma("pool", MT[s_ * 16:(s_ + 1) * 16, g0:g0 + 16, s_ * 16:128], KKs[0:16, :, 0:(8 - s_) * 16])
        return sm

    SM = s5_setup()

    def load_ln(i):
        dma("sp", lnp[:, 0, :], ln_g[i].partition_broadcast(128))
        dma("sp", lnp[:, 1, :], ln_b[i].partition_broadcast(128))

    lnrot = [0]

    class LNPipe:
        def __init__(self):
            self.q = []

        def add(self, t, eps):
            r = lnrot[0]
            lnrot[0] = (r + 1) % 8
            e = dict(t=t, r=r, age=1, s2a=False, bk=None)
            S.op("dve", lambda: nc.vector.bn_stats(stats[:, r, 0:6], X[:, t, 0:512]),
                 reads=[X[:, t, 0:512]], writes=[stats[:, r, 0:6]])
            S.op("dve", lambda: nc.vector.bn_stats(stats[:, r, 6:12], X[:, t, 512:1024]),
                 reads=[X[:, t, 512:1024]], writes=[stats[:, r, 6:12]])
            S.op("dve", lambda: nc.vector.bn_aggr(mv[:, r, :], stats[:, r, :]),
                 reads=[stats[:, r, :]], writes=[mv[:, r, :]])
            ts(rstd[:, r, 0:1], mv[:, r, 1:2], eps, None, ALU.add)
            ptt(rstd[:, r, 0:1], rstd[:, r, 0:1], neghalf[:], ALU.pow)
            self.q.append(e)

        def pre(self):
            for e in list(self.q):
                t, r = e["t"], e["r"]
                xt = X[:, t, :]
                if e["age"] == 3:
                    acopy(xT[:, :, t * 128:(t + 1) * 128], e["bk"].rearrange("p (k t) -> p k t", k=8))
                    self.q.remove(e)
                elif e["age"] == 2:
                    tt(xt, xt, lnp[:, 0, :], ALU.mult)
                    tt(xt, xt, lnp[:, 1, :], ALU.add)
                    acopy(Xb[:], xt)
                    e["s2a"] = True
                elif e["age"] == 1:
                    ts(rstd[:, r, 1:2], mv[:, r, 0:1], rstd[:, r, 0:1], -1.0, ALU.mult, ALU.mult)
                    act(xt, xt, AF.Identity, bias=rstd[:, r, 1:2], scale=rstd[:, r, 0:1])
                    e["age"] = 2

        def mid(self):
            for e in self.q:
                if e["age"] == 2 and e["s2a"]:
                    bk = bank().bitcast(BF16)
                    for kc in range(8):
                        tr(bk[:, kc * 128:(kc + 1) * 128], Xb[:, kc * 128:(kc + 1) * 128], identb[:])
                    e["bk"] = bk
                    e["age"] = 3

        def flush(self):
            while self.q:
                self.pre()
                self.mid()

        def flush_fns(self):
            for e in list(self.q):
                if e["age"] == 3:
                    t = e["t"]
                    acopy(xT[:, :, t * 128:(t + 1) * 128], e["bk"].rearrange("p (k t) -> p k t", k=8))
                    self.q.remove(e)
            return [self.flush]

    def to_xT(t):
        acopy(Xb[:], X[:, t, :])
        bk = bank().bitcast(BF16)
        for kc in range(8):
            tr(bk[:, kc * 128:(kc + 1) * 128], Xb[:, kc * 128:(kc + 1) * 128], identb[:])
        vcopy(xT[:, :, t * 128:(t + 1) * 128], bk.rearrange("p (k t) -> p k t", k=8))

    def ffn(w_in, w_out, ln_i, hook=None):
        load_ln(ln_i)
        for hf in range(2):
            pipe = LNPipe()
            for pr in range(6):
                fcs = [f for f in (hf * 11 + 2 * pr, hf * 11 + 2 * pr + 1) if f < hf * 11 + 11]
                nf = len(fcs)
                c0 = fcs[0] * 128
                sl = wload([
                    ((lambda sl, nf=nf: sl[:, 0:2048].rearrange("p (k n) -> p k n", k=8)[:, :, 0:nf * 128]),
                     wview(w_in, c0, nf * 128)),
                    ((lambda sl, nf=nf: sl[:, 2048:4096].rearrange("p (k n) -> p k n", k=8)[:, :, 0:nf * 128]),
                     wview(w_in, DFF + c0, nf * 128))])
                gv = sl[:, 0:2048].rearrange("p (k n) -> p k n", k=8)
                uv = sl[:, 2048:4096].rearrange("p (k n) -> p k n", k=8)
                if pr == 2:
                    wov = w_out.rearrange("(f k) d -> k f d", k=128)
                    for q in range(0, 11, 4):
                        n = min(4, 11 - q)
                        dma("pool", WO[:, q:q + n, :], wov[:, hf * 11 + q:hf * 11 + q + n, :])
                for j, fc in enumerate(fcs):
                    fl = fc - hf * 11
                    for thf in range(2):
                        bg = bank(); bu = bank()
                        for kc in range(8):
                            mm(bg, gv[:, kc, j * 128:(j + 1) * 128], xT[:, kc, thf * 512:(thf + 1) * 512],
                               start=(kc == 0), stop=(kc == 7))
                        for kc in range(8):
                            mm(bu, uv[:, kc, j * 128:(j + 1) * 128], xT[:, kc, thf * 512:(thf + 1) * 512],
                               start=(kc == 0), stop=(kc == 7))
                        rr = (fl * 2 + thf) % 2
                        act(sgt[:, rr, :], bg, AF.Silu)
                        tt(gT[:, fl, thf * 512:(thf + 1) * 512], bu, sgt[:, rr, :], ALU.mult)
                sfree(sl)
            if hf == 1 and hook is not None:
                hook()
            for t in range(8):
                if hf == 1:
                    pipe.pre()
                bks = [bank(), bank()]
                for dh in range(2):
                    for fl in range(11):
                        mm(bks[dh], gT[:, fl, t * 128:(t + 1) * 128], WO[:, fl, dh * 512:(dh + 1) * 512],
                           start=(fl == 0), stop=(fl == 10))
                if hf == 1:
                    pipe.mid()
                for dh in range(2):
                    xs = X[:, t, dh * 512:(dh + 1) * 512]
                    if hf == 0:
                        stt(xs, xs, 2.0 * ALPHA, bks[dh], ALU.mult, ALU.add)
                    else:
                        tt(xs, xs, bks[dh], ALU.add)
                if hf == 1:
                    pipe.add(t, 4.0 * EPS)
            pipe.flush()

    def mix_half(h, first, deferred=()):
        T0 = 4 * h
        c0 = h * 512
        s_za = v8(wload([(v8, wview(wmi, 0, 512))]))
        s_zu = v8(wload([(v8, wview(wmi, 512, 512))]))
        s_zv = v8(wload([(v8, wview(wmi, 1024, 512))]))
        for s_ in range(8):
            bk = bank()
            for kc in range(8):
                mm(bk[0:64, :], xT[:, kc, c0 + s_:c0 + 512:8], s_za[:, kc, :], start=(kc == 0), stop=(kc == 7))
            acopy(Ub[0:64, :, s_, :], bk[0:64, :].rearrange("p (g j) -> p g j", g=32))
        sfree(s_za)
        for g2 in range(2):
            bk = bank().bitcast(BF16)
            for gl in range(16):
                g = g2 * 16 + gl
                tr(bk[:, gl * 64:(gl + 1) * 64], Ub[0:64, g, :, :].rearrange("p s j -> p (s j)"),
                   identb[0:64, 0:64])
            acopy(UT[:, g2 * 16:(g2 + 1) * 16, :], bk.rearrange("p (g b) -> p g b", g=16))
        for q in range(8):
            bk = bank()
            for gl in range(4):
                g = q * 4 + gl
                for c in range(2):
                    col = (gl * 2 + c) * 64
                    mm(bk[0:64, col:col + 64], MI[:, g, c, :], UT[:, g, :], start=True, stop=True)
            bv = bk[0:64, :].rearrange("p (g c b) -> p g c b", g=4, c=2)
            for c in range(2):
                acopy(XS[0:64, :, c, q * 4:q * 4 + 4].rearrange("p b g -> p g b"), bv[:, :, c, :])
        for cc in range(4):
            bk = bank()
            for kc in range(8):
                mm(bk, s_zu[:, kc, cc * 128:(cc + 1) * 128], xT[:, kc, c0:c0 + 512], start=(kc == 0), stop=(kc == 7))
            act(uT[:, cc, :], bk, AF.Gelu)
        sfree(s_zu)
        for q in range(2):
            s_ga = v8(wload([(v8, wview(wmi, 1536 + q * 512, 512))]))
            for dl in range(4):
                dc = q * 4 + dl
                bk = bank()
                for kc in range(8):
                    mm(bk, s_ga[:, kc, dl * 128:(dl + 1) * 128], xT[:, kc, c0:c0 + 512], start=(kc == 0), stop=(kc == 7))
                act(mT[:, dc, :], bk, AF.Sigmoid)
            sfree(s_ga)
        QW = SM["QWv"]
        if first:
            vmemset(carry[:], 0.0)
        KS = 64 // MB
        XSv = XS[0:64]
        rt1 = av(45056, 4096, F32).rearrange("p (k n) -> p k n", k=16)

        def arv(j, K):
            return QW[:, j, 0, :].unsqueeze(1).unsqueeze(1).broadcast_to([64, K, 2, 32])

        def aiv(j, K):
            return AIJ[:, j - 1, :].rearrange("p (c g) -> p c g", c=2).unsqueeze(1).broadcast_to([64, K, 2, 32])

        def cmul_add(dst, src, j, K):
            t1 = rt1[0:64, 0:K, :].rearrange("p k (c g) -> p k c g", c=2)
            t2 = rt2[:, 0:K, :].rearrange("p k (c g) -> p k c g", c=2)
            tt(t1, arv(j, K), src, ALU.mult)
            tt(t2, aiv(j, K), src[:, :, ::-1, :], ALU.mult)
            tt(dst, dst, t1, ALU.add)
            tt(dst, dst, t2, ALU.add)

        def rec_prefix():
            for i in range(1, MB):
                for k0 in range(0, KS, 16):
                    K = min(16, KS - k0)
                    src = XSv[:, (k0 * MB + i - 1):(k0 + K) * MB:MB, :, :]
                    dst = XSv[:, (k0 * MB + i):(k0 + K) * MB:MB, :, :]
                    cmul_add(dst, src, 1, K)

        def rec_seq(k0, k1):
            ar = QW[:, MB, 0, :].unsqueeze(1).broadcast_to([64, 2, 32])
            ai = AIJ[:, MB - 1, :].rearrange("p (c g) -> p c g", c=2)
            t1 = T1s[:].rearrange("p (c g) -> p c g", c=2)
            t2 = T2s[:].rearrange("p (c g) -> p c g", c=2)
            for k in range(k0, k1):
                prev = carry[:].rearrange("p (c g) -> p c g", c=2) if k == 0 else XSv[:, k * MB - 1, :, :]
                cur_ = XSv[:, k * MB + MB - 1, :, :]
                tt(t1, ar, prev, ALU.mult)
                tt(t2, ai, prev[:, ::-1, :], ALU.mult)
                tt(t1, t1, cur_, ALU.add)
                tt(cur_, t1, t2, ALU.add)

        def rec_fill():
            for i in range(MB - 1):
                j = i + 1
                cmul_add(XSv[:, i:i + 1, :, :], carry[:].rearrange("p (k c g) -> p k c g", k=1, c=2), j, 1)
                for k0 in range(1, KS, 16):
                    K = min(16, KS - k0)
                    src = XSv[:, (k0 * MB - 1):(k0 + K) * MB - 1:MB, :, :]
                    dst = XSv[:, (k0 * MB + i):(k0 + K) * MB:MB, :, :]
                    cmul_add(dst, src, j, K)
            for c in range(2):
                acopy(Hb[0:64, c, :, 1:65], XSv[:, :, c, :].rearrange("p b g -> p g b"))
            acopy(Hb[0:64, :, :, 0], carry[:].rearrange("p (c g) -> p c g", c=2))
            vcopy(carry[:].rearrange("p (c g) -> p c g", c=2), XSv[:, 63, :, :])

        rec_plan = [rec_prefix, (lambda: rec_seq(0, KS // 2)), (lambda: rec_seq(KS // 2, KS)), rec_fill]

        def rec_steps(b0, b1):
            rec_plan[b0 // 16]()

        for fn_ in deferred:
            fn_()
        def gmlp_tile(tl):
            t = T0 + tl
            bk = bank()
            for kc in range(8):
                mm(bk, xT[:, kc, t * 128:(t + 1) * 128], s_zv[:, kc, :], start=(kc == 0), stop=(kc == 7))
            r = tl % 2
            act(zv[:, r, :], bk, AF.Gelu)
            lr = lnrot[0]
            lnrot[0] = (lr + 1) % 8
            S.op("dve", (lambda r=r, lr=lr: nc.vector.bn_stats(stats[:, lr, 0:6], zv[:, r, :])),
                 reads=[zv[:, r, :]], writes=[stats[:, lr, 0:6]])
            S.op("dve", (lambda lr=lr: nc.vector.bn_aggr(mv[:, lr, :], stats[:, lr, 0:6])),
                 reads=[stats[:, lr, 0:6]], writes=[mv[:, lr, :]])
            ts(rstd[:, lr, 0:1], mv[:, lr, 1:2], EPS, None, ALU.add)
            ptt(rstd[:, lr, 0:1], rstd[:, lr, 0:1], neghalf[:], ALU.pow)
            ts(zv[:, r, :], zv[:, r, :], mv[:, lr, 0:1], rstd[:, lr, 0:1], ALU.subtract, ALU.mult)
            tt(zv[:, r, :], zv[:, r, :], gbc[:, 0, :], ALU.mult)
            z4 = zv[:, r, :].rearrange("p (q e d) -> p q e d", q=4, e=2)
            b4 = gbc[:, 1, :].rearrange("p (q e d) -> p q e d", q=4, e=2)
            A_ = ABv[:, r, :].rearrange("p (q e d) -> p q e d", q=4, e=2)
            B_ = ABv[:, 2 + r, :].rearrange("p (q e d) -> p q e d", q=4, e=2)
            tt(A_[:, :, 0, :], z4[:, :, 0, :], b4[:, :, 0, :], ALU.add)
            tt(B_[:, :, 1, :], z4[:, :, 1, :], b4[:, :, 1, :], ALU.add)
            bk = bank()
            for cc in range(4):
                o = bk[:, cc * 128:(cc + 1) * 128]
                mm(o, ABv[:, r, cc * 128:(cc + 1) * 128], WS[:, 2 * cc, :], start=True, stop=False)
                mm(o, ABv[:, 2 + r, cc * 128:(cc + 1) * 128], WS[:, 2 * cc + 1, :], start=False, stop=False)
                mm(o, hsel[0:40, cc * 128:(cc + 1) * 128], bsT[0:40, :], start=False, stop=True)
            tt(ybpre[:, :, tl * 128:(tl + 1) * 128], bk.rearrange("p (c t) -> p c t", c=4),
               uT[:, :, tl * 128:(tl + 1) * 128], ALU.mult)

        s_upb = v4(wload([(v4, up_b.rearrange("(kc k) n -> k kc n", k=128))]))

        def yb_part(dcs):
            for dc in dcs:
                if dc % 4 == 0:
                    s_gb[0] = v8(wload([(v8, wview(wmi, 2560 + dc * 128, 512))]))
                b_yb = bank(); b_gb = bank()
                for kc in range(4):
                    mm(b_yb, s_upb[:, kc, dc * 128:(dc + 1) * 128], ybpre[:, kc, :], start=(kc == 0), stop=(kc == 3))
                o = (dc % 4) * 128
                for kc in range(8):
                    mm(b_gb, s_gb[0][:, kc, o:o + 128], xT[:, kc, c0:c0 + 512], start=(kc == 0), stop=(kc == 7))
                act(zv[:, dc % 2, :], b_gb, AF.Sigmoid)
                tt(m2[:, dc, :], b_yb, zv[:, dc % 2, :], ALU.mult)
                if dc % 4 == 3:
                    sfree(s_gb[0])

        s_gb = [None]
        rec_steps(0, 16)
        gmlp_tile(0)
        gmlp_tile(1)
        rec_steps(16, 32)
        gmlp_tile(2)
        gmlp_tile(3)
        sfree(s_zv)
        rec_steps(32, 48)
        yb_part(range(0, 4))
        rec_steps(48, 64)
        yb_part(range(4, 8))
        sfree(s_upb)
        for q in range(8):
            bk = bank()
            for gl in range(4):
                g = q * 4 + gl
                o = bk[0:64, gl * 128:(gl + 1) * 128]
                mm(o, UT[:, g, :], MT[:, g, :], start=True, stop=False)
                mm(o, Hb[0:64, 0, g, 0:64], MO[:, 0, g, :], start=False, stop=False)
                mm(o, Hb[0:64, 1, g, 0:64], MO[:, 1, g, :], start=False, stop=True)
            act(yg[0:64, :, q * 64:(q + 1) * 64].rearrange("p s (g i) -> p s g i", g=4),
                bk[0:64, :].rearrange("p (g s i) -> p s g i", g=4, s=8), AF.Gelu)
        for c2 in range(2):
            bk = bank().bitcast(BF16)
            for ci_ in range(2):
                cc = c2 * 2 + ci_
                for s_ in range(8):
                    tr(bk[:, ci_ * 512 + s_ * 64:ci_ * 512 + s_ * 64 + 64], yg[0:64, s_, cc * 128:(cc + 1) * 128],
                       identb[0:64, 0:64])
            acopy(ygT[:, c2 * 2:c2 * 2 + 2, :].rearrange("p c (b s) -> p c s b", s=8),
                  bk.rearrange("p (c s b) -> p c s b", c=2, s=8))
        vglu = (lambda sl: sl[:, 0:2048].rearrange("p (k n) -> p k n", k=4))
        s_glu = vglu(wload([(vglu, glu_w.rearrange("(kc k) n -> k kc n", k=128))]))
        s_upa = v4(wload([(v4, up_a.rearrange("(kc k) n -> k kc n", k=128))]))
        for cc in range(4):
            bk = bank()
            for kc in range(4):
                mm(bk, s_glu[:, kc, cc * 128:(cc + 1) * 128], ygT[:, kc, :], start=(kc == 0), stop=(kc == 3))
            r = cc % 4
            act(mtmp[:, r, :], bk, AF.Sigmoid, bias=glub[:, cc:cc + 1])
            tt(yain[:, cc, :], ygT[:, cc, :], mtmp[:, r, :], ALU.mult)
        sfree(s_glu)
        for dc in range(8):
            b_ya = bank()
            for kc in range(4):
                mm(b_ya, s_upa[:, kc, dc * 128:(dc + 1) * 128], yain[:, kc, :], start=(kc == 0), stop=(kc == 3))
            r = dc % 4
            tt(mtmp[:, r, :], b_ya, mT[:, dc, :], ALU.mult)
            tt(mT[:, dc, :], mtmp[:, r, :], m2[:, dc, :], ALU.add)
        sfree(s_upa)
        s_wo = []
        pipe2 = LNPipe()
        for dh in range(2):
            s_wo.append(v8(wload([(v8, wview(wmo, dh * 512, 512))])))
        for tl in range(4):
            t = T0 + tl
            pipe2.pre()
            bks = [bank(), bank()]
            for dh in range(2):
                for kc in range(8):
                    mm(bks[dh], mT[:, kc, tl * 128:(tl + 1) * 128], s_wo[dh][:, kc, :], start=(kc == 0), stop=(kc == 7))
            pipe2.mid()
            for dh in range(2):
                xs = X[:, t, dh * 512:(dh + 1) * 512]
                stt(xs, xs, ALPHA, bks[dh], ALU.mult, ALU.add)
            pipe2.add(t, EPS)
        sfree(s_wo[0])
        sfree(s_wo[1])
        return pipe2.flush_fns()

    ptall = av(49152, 8192, F32).rearrange("p (t n) -> p t n", t=8)
    pTall = av(57344, 4096, BF16).rearrange("p (t k n) -> p t k n", t=8, k=2)
    pbt2 = av(61440, 1024, BF16).rearrange("p (r n) -> p r n", r=2)
    sgp4 = av(0, 8192, F32).rearrange("p (r n) -> p r n", r=4)

    def ple_prep(st):
        for t in range(8):
            r0 = st * 1024 + t * 128
            dma("sp", ptall[:, t, :], p_d[r0:r0 + 128, :])
        for t4 in range(2):
            bk = bank().bitcast(BF16)
            for tl in range(4):
                t = t4 * 4 + tl
                acopy(pbt2[:, t % 2, :], ptall[:, t, :])
                for kc in range(2):
                    tr(bk[:, (tl * 2 + kc) * 128:(tl * 2 + kc + 1) * 128], pbt2[:, t % 2, kc * 128:(kc + 1) * 128],
                       identb[:])
            vcopy(pTall[:, t4 * 4:t4 * 4 + 4, :, :], bk.rearrange("p (t k n) -> p t k n", t=4, k=2))

    def ple(st):
        s_pg = []
        for dh in range(2):
            s_pg.append(v8(wload([(v8, wview(wpg, dh * 512, 512))])))
        vpp = (lambda sl: sl[:, 0:2048].rearrange("p (k n) -> p k n", k=2))
        s_pp = vpp(wload([(vpp, wpp.rearrange("(kc k) n -> k kc n", k=128))]))
        for t in range(8):
            r0 = st * 1024 + t * 128
            for dh in range(2):
                bg = bank(); bp = bank()
                for kc in range(8):
                    mm(bg, xT[:, kc, t * 128:(t + 1) * 128], s_pg[dh][:, kc, :], start=(kc == 0), stop=(kc == 7))
                for kc in range(2):
                    mm(bp, pTall[:, t, kc, :], s_pp[:, kc, dh * 512:(dh + 1) * 512], start=(kc == 0), stop=(kc == 1))
                r = (t * 2 + dh) % 4
                act(sgp4[:, r, :], bg, AF.Sigmoid)
                tt(sgp4[:, r, :], bp, sgp4[:, r, :], ALU.mult)
                xs = X[:, t, dh * 512:(dh + 1) * 512]
                tt(xs, xs, sgp4[:, r, :], ALU.add)
            dma("sp", out_d[r0:r0 + 128, :], X[:, t, :])
        sfree(s_pg[0]); sfree(s_pg[1]); sfree(s_pp)

    class _Stop(Exception):
        pass

    def stage(name):
        if dbg and dbg.get("stop") == name:
            raise _Stop()

    def emit_all():
      try:
          stage("setup")
          for st in range(nst):
              for t in range(8):
                  r0 = st * 1024 + t * 128
                  dma("sp", X[:, t, :], x_d[r0:r0 + 128, :])
              for t in range(8):
                  to_xT(t)
              stage("xT%d" % st)
              ffn(w1i, w1o, 0)
              stage("ffn1_%d" % st)
              load_ln(1)
              dfr = []
              for h in range(2):
                  dfr = mix_half(h, first=(st % 2 == 0 and h == 0), deferred=dfr)
                  stage("mix%d_%d" % (h, st))
              for fn_ in dfr:
                  fn_()
              ffn(w2i, w2o, 2, hook=(lambda st=st: ple_prep(st)))
              stage("ffn2_%d" % st)
              ple(st)
      except _Stop:
        pass


    class _Dummy:
        def op(self, *a, **k):
            return None

    S_real = S
    saved = (pbank[0], lnrot[0], list(free_slots))
    S = _Dummy()
    S.h = S_real.h
    emit_all()
    pbank[0], lnrot[0] = saved[0], saved[1]
    free_slots[:] = saved[2]
    S = S_real
    W["mode"] = "play"
    pump()
    emit_all()

    if dbg:
        lv = dict(X=X[:], xT=xT[:], MI=MI[:], MT=MT[:], MO=MO[:], WS=WS[:], bsT=bsT[:], ARt=ARt[:], AIt=AIt[:],
                  gT=gT, UT=UT[:, :, :], XS=XS[0:64], Hb=Hb[0:64], yg=yg[0:64], ygT=ygT, yain=yain, uT=uT,
                  ybpre=ybpre, mT=mT, Ub=Ub[0:64], dcol=dcol[:], glub=glub[:])
        for k_, v_ in SM.items():
            lv['s5_' + k_] = v_[:]
        for name in dbg.get("dump", []):
            ap = lv[name]
            dt_ = nc.dram_tensor("dbg_" + name, list(ap.shape), ap.dtype, kind="ExternalOutput").ap()
            dma("sp", dt_, ap)
    print('sbuf bytes remaining', nc.sbuf_bytes_remaining)
    S.emit()
    return nc


_NC = {}


def _consts():
    identf = np.eye(128, dtype=np.float32)
    mask = np.triu(np.ones((128, 128), dtype=np.float32))
    hsel = np.zeros((40, 512), dtype=np.float32)
    for h in range(8):
        hsel[h, h * 64:(h + 1) * 64] = 1.0
        hsel[32 + h, h * 64:(h + 1) * 64] = 1.0
    return dict(c_identf=identf, c_mask=mask, c_hsel=hsel)


def kernel(**inputs):
    n = 8
    if "nc" not in _NC:
        _NC["nc"] = build(4)
    nc = _NC["nc"]
    x = np.ascontiguousarray(np.asarray(inputs["x"], dtype=np.float32))
    p = np.ascontiguousarray(np.asarray(inputs["p"], dtype=np.float32))
    shared = {}
    for k, v in inputs.items():
        if k in ("x", "p"):
            continue
        a = np.asarray(v, dtype=np.float32)
        shared[k] = np.ascontiguousarray(a[0])
    shared.update(_consts())
    in_maps = []
    for c in range(n):
        m = dict(shared)
        m["x"] = x[2 * c:2 * c + 2].reshape(NTOK, D)
        m["p"] = p[0, 2 * c:2 * c + 2].reshape(NTOK, 256)
        in_maps.append(m)
    res = run_bass_kernel_spmd(nc, in_maps, core_ids=list(range(n)))
    out = np.stack([np.asarray(r["out"], dtype=np.float32).reshape(2, 2048, D) for r in res.results], axis=0)
    return out.reshape(16, 2048, D)
```

```python
import math
import numpy as np
import ml_dtypes
import concourse.bass as bass
import concourse.mybir as mybir
from concourse.bass_utils import run_bass_kernel_spmd

F32 = mybir.dt.float32
BF16 = mybir.dt.bfloat16
AF = mybir.ActivationFunctionType
ALU = mybir.AluOpType

D = 1024
DFF = 2816
NFC = 22
DIN = 3584
NTOK = 4096
ALPHA = 2.0 ** 0.25
EPS = 1e-5
NSLOT = 4
NDSEM = 48


def _esize(dt):
    return 4 if dt == F32 else 2


class Sched:
    ENG = ("pe", "act", "dve", "pool", "sp")
    G = 256

    def __init__(self, nc):
        self.nc = nc
        self.h = dict(pe=nc.tensor, act=nc.scalar, dve=nc.vector, pool=nc.gpsimd, sp=nc.sync)
        self.ops = []
        self.slots = {}
        self.seen = {e: {} for e in self.ENG}
        self.dnext = {"sp": 0, "pool": NDSEM // 2}
        self.dval = [0] * NDSEM
        self.dlast = [None] * NDSEM

    def _slots(self, ap):
        name = ap.name
        es = _esize(ap.dtype)
        dims = list(ap.ap)
        pstep = dims[0][0]
        off = ap.offset
        if pstep > 0:
            off = off % pstep
        fd = dims[1:]
        ispsum = "PSUM" in str(ap.space).upper()
        g = 2048 if ispsum else self.G
        if not fd:
            lo = off * es
            return [(name, lo // g)]
        base = off
        nd = []
        for st, cnt in fd:
            if st < 0:
                base += st * (cnt - 1)
                st = -st
            nd.append((st, cnt))
        inner = nd[-1]
        outer = nd[:-1]
        nout = 1
        for st, cnt in outer:
            nout *= cnt
        res = set()
        if nout > 512:
            hi = base + sum(st * (cnt - 1) for st, cnt in nd)
            for s in range((base * es) // g, (hi * es + es - 1) // g + 1):
                res.add((name, s))
            return res
        idx = [0] * len(outer)
        while True:
            o = base + sum(i * st for i, (st, cnt) in zip(idx, outer))
            lo = o * es
            hi = (o + inner[0] * (inner[1] - 1)) * es + es - 1
            for s in range(lo // g, hi // g + 1):
                res.add((name, s))
            k = len(outer) - 1
            while k >= 0:
                idx[k] += 1
                if idx[k] < outer[k][1]:
                    break
                idx[k] = 0
                k -= 1
            if k < 0:
                break
        return res

    def op(self, eng, fn, reads=(), writes=(), dma=False):
        deps_e = {}
        deps_d = {}

        def add(tok):
            if tok is None:
                return
            if tok[0] == "e":
                if tok[1] == eng and eng == "pe":
                    return
                deps_e[tok[1]] = max(deps_e.get(tok[1], -1), tok[2])
            else:
                deps_d[tok[1]] = max(deps_d.get(tok[1], 0), tok[2])

        rs = set()
        for ap in reads:
            rs |= set(self._slots(ap))
        ws = set()
        for ap in writes:
            ws |= set(self._slots(ap))
        for s in rs:
            st = self.slots.get(s)
            if st is not None:
                add(st[0])
        for s in ws:
            st = self.slots.get(s)
            if st is not None:
                add(st[0])
                for t in st[1].values():
                    add(t)
                for t in st[2]:
                    add(t)
        idx = len(self.ops)
        rec = dict(eng=eng, fn=fn, waits=[], inc=False, dsem=None)
        if dma:
            k = self.dnext[eng]
            base = 0 if eng == "sp" else NDSEM // 2
            self.dnext[eng] = base + (k - base + 1) % (NDSEM // 2)
            if self.dlast[k] is not None:
                add(("d", k, self.dlast[k]))
            self.dval[k] += 16
            self.dlast[k] = self.dval[k]
            rec["dsem"] = k
            tok = ("d", k, self.dval[k])
        else:
            tok = ("e", eng, idx)
        seen = self.seen[eng]
        for f, i in deps_e.items():
            if seen.get(("e", f), -1) >= i:
                continue
            seen[("e", f)] = i
            rec["waits"].append(("e", f, i))
        for k, v in deps_d.items():
            if seen.get(("d", k), 0) >= v:
                continue
            seen[("d", k)] = v
            rec["waits"].append(("d", k, v))
        self.ops.append(rec)
        for s in rs:
            st = self.slots.get(s)
            if st is None:
                st = [None, {}, []]
                self.slots[s] = st
            if tok[0] == "e":
                st[1][eng] = tok
            else:
                st[2].append(tok)
        for s in ws:
            self.slots[s] = [tok, {}, []]
        return idx

    def emit(self):
        nc = self.nc
        for rec in self.ops:
            for w in rec["waits"]:
                if w[0] == "e":
                    self.ops[w[2]]["inc"] = True
        esem = {e: nc.alloc_semaphore(name="es_" + e) for e in self.ENG}
        dsem = [nc.alloc_semaphore(name="ds_%d" % i) for i in range(NDSEM)]
        cnt = {e: 0 for e in self.ENG}
        val = {}
        for i, rec in enumerate(self.ops):
            if rec["inc"]:
                cnt[rec["eng"]] += 1
                val[i] = cnt[rec["eng"]]
        for i, rec in enumerate(self.ops):
            h = self.h[rec["eng"]]
            for w in rec["waits"]:
                if w[0] == "e":
                    h.wait_ge(esem[w[1]], val[w[2]])
                else:
                    h.wait_ge(dsem[w[1]], w[2])
            ins = rec["fn"]()
            if rec["dsem"] is not None:
                ins.then_inc(dsem[rec["dsem"]], 16)
            elif rec["inc"]:
                ins.then_inc(esem[rec["eng"]], 1)
        for k in range(NDSEM):
            if self.dlast[k] is not None:
                nc.sync.wait_ge(dsem[k], self.dlast[k])


def build(nst=4, dbg=None):
    nc = bass.Bass("TRN2", target_bir_lowering=False)
    S = Sched(nc)
    ntok = nst * 1024

    def din(name, shape):
        return nc.dram_tensor(name, list(shape), F32, kind="ExternalInput").ap()

    x_d = din("x", [NTOK, D])
    p_d = din("p", [NTOK, 256])
    w1i = din("ffn1_w_in", [D, 2 * DFF]); w1o = din("ffn1_w_out", [DFF, D])
    w2i = din("ffn2_w_in", [D, 2 * DFF]); w2o = din("ffn2_w_out", [DFF, D])
    ln_g = [din("ln1_g", [D]), din("ln2_g", [D]), din("ln3_g", [D])]
    ln_b = [din("ln1_b", [D]), din("ln2_b", [D]), din("ln3_b", [D])]
    wmi = din("mix_w_in", [D, DIN])
    lam_re = din("ssm_lambda_re", [32, 64]); lam_im = din("ssm_lambda_im", [32, 64])
    log_dt = din("ssm_log_dt", [32])
    b_re = din("ssm_b_re", [32, 64, 16]); b_im = din("ssm_b_im", [32, 64, 16])
    c_re = din("ssm_c_re", [32, 16, 64]); c_im = din("ssm_c_im", [32, 16, 64])
    ssm_d = din("ssm_d", [512])
    glu_w = din("ssm_glu_w", [512, 512]); glu_b = din("ssm_glu_b", [512])
    gm_g = din("gmlp_ln_g", [512]); gm_b = din("gmlp_ln_b", [512])
    gm_ws = din("gmlp_w_s", [8, 128, 128]); gm_bs = din("gmlp_b_s", [8, 128])
    up_a = din("up_a", [512, D]); up_b = din("up_b", [512, D])
    wmo = din("mix_w_out", [D, D])
    wpp = din("ple_w_proj", [256, D]); wpg = din("ple_w_gate", [D, D])
    c_identf = din("c_identf", [128, 128])
    c_mask = din("c_mask", [128, 128])
    c_hsel = din("c_hsel", [40, 512])
    out_d = nc.dram_tensor("out", [NTOK, D], F32, kind="ExternalOutput").ap()
    dbg_out = {}

    def sb(name, shape, dt):
        return nc.alloc_sbuf_tensor(name, list(shape), dt)

    X = sb("X", [128, 8, 1024], F32)
    xT = sb("xT", [128, 8, 1024], BF16)
    Xb = sb("Xb", [128, 1024], BF16)
    ring = sb("ring", [128, NSLOT, 4096], BF16)
    lnp = sb("lnp", [128, 2, 1024], F32)
    MI = sb("MI", [128, 32, 2, 64], BF16)
    MT = sb("MT", [128, 32, 128], BF16)
    MO = sb("MO", [64, 2, 32, 128], BF16)
    WS = sb("WS", [128, 8, 128], BF16)
    gbc = sb("gbc", [128, 2, 512], F32)
    hsel = sb("hsel", [40, 512], BF16)
    bsT = sb("bsT", [40, 128], BF16)
    identf = sb("identf", [128, 128], F32)
    identb = sb("identb", [128, 128], BF16)
    maskf = sb("maskf", [128, 128], F32)
    ARt = sb("ARt", [64, 64], F32)
    AIt = sb("AIt", [64, 64], F32)
    MB = 4
    AIJ = sb("AIJ", [64, MB, 64], F32)
    carry = sb("carry", [64, 64], F32)
    T1s = sb("T1s", [64, 64], F32)
    rt2 = sb("rt2", [64, 16, 64], F32)
    T2s = sb("T2s", [64, 64], F32)
    dcol = sb("dcol", [128, 32], F32)
    glub = sb("glub", [128, 4], F32)
    stats = sb("stats", [128, 8, 12], F32)
    mv = sb("mv", [128, 8, 2], F32)
    rstd = sb("rstd", [128, 8, 2], F32)
    neghalf = sb("neghalf", [128, 1], F32)
    AREN = 62464
    arena = sb("arena", [128, AREN // 2], BF16)
    psum = nc.alloc_psum_tensor("psum", [128, 8, 512], F32)

    def av(lo, nbytes, dt, shape=None):
        es = _esize(dt)
        assert lo % 4 == 0 and lo + nbytes <= AREN, (lo, nbytes)
        v = arena[:, lo // 2:(lo + nbytes) // 2]
        if dt == F32:
            v = v.bitcast(F32)
        return v

    gT = av(0, 22528, BF16).rearrange("p (f t) -> p f t", f=11)
    WO = av(22528, 22528, BF16).rearrange("p (f d) -> p f d", f=11)
    sgt = av(45056, 4096, F32).rearrange("p (r n) -> p r n", r=2)
    Ub = av(0, 8192, BF16).rearrange("p (g s j) -> p g s j", g=32, s=8)
    yg = av(20480, 8192, BF16).rearrange("p (s c) -> p s c", s=8)
    m2 = av(0, 8192, BF16).rearrange("p (c t) -> p c t", c=8)
    UT = av(8192, 4096, BF16).rearrange("p (g b) -> p g b", g=32)
    XS = av(12288, 16384, F32).rearrange("p (b c g) -> p b c g", b=64, c=2)
    ygT = av(12288, 4096, BF16).rearrange("p (c t) -> p c t", c=4)
    yain = av(16384, 4096, BF16).rearrange("p (c t) -> p c t", c=4)
    mtmp = av(20480, 8192, F32).rearrange("p (r n) -> p r n", r=4)
    Hb = av(28672, 8448, BF16).rearrange("p (c g s) -> p c g s", c=2, g=32)
    uT = av(37888, 4096, BF16).rearrange("p (c t) -> p c t", c=4)
    zv = av(41984, 4096, F32).rearrange("p (r n) -> p r n", r=2)
    ABv = sb("ABv", [128, 4, 512], BF16)
    ybpre = av(50176, 4096, BF16).rearrange("p (c t) -> p c t", c=4)
    mT = av(54272, 8192, BF16).rearrange("p (c t) -> p c t", c=8)

    pbank = [0]

    def bank():
        b = pbank[0]
        pbank[0] = (b + 1) % 8
        return psum[:, b, :]

    def mm(out, lhsT, rhs, start, stop):
        S.op("pe", lambda: nc.tensor.matmul(out, lhsT, rhs, start=start, stop=stop),
             reads=[lhsT, rhs], writes=[out])

    def tr(out, in_, ident):
        S.op("pe", lambda: nc.tensor.transpose(out, in_, ident), reads=[in_, ident], writes=[out])

    def act(out, in_, func, bias=None, scale=None):
        rd = [in_]
        kw = {}
        if bias is not None:
            kw["bias"] = bias
            if not isinstance(bias, (int, float)):
                rd.append(bias)
        if scale is not None:
            kw["scale"] = scale
            if not isinstance(scale, (int, float)):
                rd.append(scale)
        S.op("act", lambda: nc.scalar.activation(out, in_, func, **kw), reads=rd, writes=[out])

    def acopy(out, in_):
        act(out, in_, AF.Copy)

    def vcopy(out, in_):
        S.op("dve", lambda: nc.vector.tensor_copy(out, in_), reads=[in_], writes=[out])

    def tt(out, in0, in1, op):
        S.op("dve", lambda: nc.vector.tensor_tensor(out, in0, in1, op), reads=[in0, in1], writes=[out])

    def ts(out, in0, s1, s2, op0, op1=None):
        rd = [in0] + [s for s in (s1, s2) if s is not None and not isinstance(s, (int, float))]
        if op1 is None:
            S.op("dve", lambda: nc.vector.tensor_scalar(out, in0, s1, None, op0), reads=rd, writes=[out])
        else:
            S.op("dve", lambda: nc.vector.tensor_scalar(out, in0, s1, s2, op0, op1), reads=rd, writes=[out])

    def stt(out, in0, sc, in1, op0, op1):
        rd = [in0, in1] + ([] if isinstance(sc, (int, float)) else [sc])
        S.op("dve", lambda: nc.vector.scalar_tensor_tensor(out, in0, sc, in1, op0, op1), reads=rd, writes=[out])

    def ptt(out, in0, in1, op):
        S.op("pool", lambda: nc.gpsimd.tensor_tensor(out, in0, in1, op), reads=[in0, in1], writes=[out])

    def vmemset(ap, v):
        S.op("dve", lambda: nc.vector.memset(ap, v), writes=[ap])

    def dma(q, out, in_, slow=False):
        sbuf_r = [in_] if str(in_.space).upper().startswith("SB") else []
        sbuf_w = [out] if str(out.space).upper().startswith("SB") else []
        h = S.h[q]
        if slow:
            S.op(q, lambda: h.dma_start(out=out, in_=in_, allow_slow_non_contiguous=True),
                 reads=sbuf_r, writes=sbuf_w, dma=True)
        else:
            S.op(q, lambda: h.dma_start(out=out, in_=in_), reads=sbuf_r, writes=sbuf_w, dma=True)

    free_slots = list(range(NSLOT))

    def slot():
        assert free_slots, "ring exhausted"
        s = free_slots.pop(0)
        return ring[:, s, :]

    W = dict(mode="rec", reqs=[], idx=0, issued=0, slots={})

    def pump():
        while free_slots and W["issued"] < len(W["reqs"]):
            sl = slot()
            parts = W["reqs"][W["issued"]]
            W["slots"][W["issued"]] = sl
            W["issued"] += 1
            for vf, src in parts:
                dma("pool", vf(sl), src)

    def wload(parts):
        if W["mode"] == "rec":
            W["reqs"].append(parts)
            return slot()
        i = W["idx"]
        W["idx"] += 1
        if W["issued"] <= i:
            pump()
        assert W["issued"] > i, "ring deadlock"
        return W["slots"].pop(i)

    def sfree(v):
        pstep = v.ap[0][0]
        s = (v.offset % pstep) // 4096
        assert s not in free_slots
        free_slots.append(s)
        if W["mode"] == "play":
            pump()

    def v8(sl):
        return sl.rearrange("p (k n) -> p k n", k=8)

    def v4(sl):
        return sl.rearrange("p (k n) -> p k n", k=4)

    def wview(w, c0, nc_):
        return w.rearrange("(kc k) n -> k kc n", k=128)[:, :, c0:c0 + nc_]

    dma("sp", identf[:], c_identf[:, :])
    dma("sp", maskf[:], c_mask[:, :])
    dma("pool", hsel[:], c_hsel[:, :])
    dma("pool", identb[:], c_identf[:, :])
    dma("sp", gbc[:, 0, :], gm_g.partition_broadcast(128))
    dma("sp", gbc[:, 1, :], gm_b.partition_broadcast(128))
    dma("sp", glub[:], glu_b.rearrange("(c p) -> p c", p=128), slow=True)
    for s_ in range(8):
        dma("sp", dcol[s_ * 16:(s_ + 1) * 16, :], ssm_d.rearrange("(g j) -> j g", j=16), slow=True)
    for r in range(4):
        vmemset(ABv[:, r, :], 0.0)
    vmemset(neghalf[:], -0.5)

    wsl = av(0, 4096, F32).rearrange("p (h s) -> p h s", h=8)
    dma("sp", wsl, gm_ws.rearrange("h t s -> t h s"))
    for hh in range(8):
        if hh % 4 == 0:
            bk = bank()
        tr(bk[:, (hh % 4) * 128:(hh % 4 + 1) * 128], wsl[:, hh, :], identf[:])
        if hh % 4 == 3:
            tt(WS[:, hh - 3:hh + 1, :], bk.rearrange("p (h t) -> p h t", h=4),
               maskf[:].unsqueeze(1).broadcast_to([128, 4, 128]), ALU.mult)
    bsf = av(4096, 512, F32)
    bsh = av(4608, 512, F32)
    vmemset(bsT[:], 0.0)
    dma("sp", bsf[0:8, :], gm_bs[:, :])
    dma("sp", bsf[32:40, :], gm_bs[:, :])
    vcopy(bsT[0:8, :], bsf[0:8, :])
    bsh2 = av(5120, 256, BF16)
    vcopy(bsh2[32:40, :], bsf[32:40, :])
    vcopy(bsh[32:40, :], bsh2[32:40, :])
    tt(bsT[32:40, :], bsf[32:40, :], bsh[32:40, :], ALU.subtract)

    def s5_setup():
        sm = {}

        bump = {"early": 0, "late": 57344}

        def t64(name, n):
            if name == "QW":
                sm[name] = sb("s5_" + name, [64, n], F32)[:]
                return sm[name]
            pool_ = "late" if name in ("PW", "Wc") else "early"
            lo = bump[pool_]
            bump[pool_] = lo + ((n * 4 + 255) // 256) * 256
            assert bump[pool_] <= (16384 if pool_ == "early" else AREN)
            sm[name] = av(lo, n * 4, F32)[0:64, :]
            return sm[name]
        lt = t64("lt", 128)[0:32, :]
        dma("sp", lt[:, 0:64], lam_re[:, :])
        dma("sp", lt[:, 64:128], lam_im[:, :])
        LR = t64("LR", 32); LI = t64("LI", 32)
        bk = bank()
        tr(bk[0:64, 0:32], lt[:, 0:64], identf[0:32, 0:32])
        tr(bk[0:64, 32:64], lt[:, 64:128], identf[0:32, 0:32])
        vcopy(LR[:], bk[0:64, 0:32])
        vcopy(LI[:], bk[0:64, 32:64])
        DT = t64("DT", 32)
        dma("sp", DT[:], log_dt.partition_broadcast(64))
        act(DT[:], DT[:], AF.Exp)
        aa = t64("aa", 32); th = t64("th", 32); mag = t64("mag", 32)
        tt(aa[:], LR[:], DT[:], ALU.mult)
        tt(th[:], LI[:], DT[:], ALU.mult)
        act(mag[:], aa[:], AF.Exp)
        sa = t64("sa", 32); ca = t64("ca", 32); tb_w = t64("tb_w", 32)
        ts(ca[:], th[:], 0.5 * math.pi, None, ALU.add)
        vcopy(sa[:], th[:])
        wtmp = t64("wtmp", 32)
        for ang in (sa, ca):
            for j in range(8):
                ts(wtmp[:], ang[:], (2 * j + 1) * math.pi, 2 * math.pi, ALU.is_gt, ALU.mult)
                if j == 0:
                    tt(tb_w[:], ang[:], wtmp[:], ALU.subtract)
                else:
                    tt(tb_w[:], tb_w[:], wtmp[:], ALU.subtract)
            vcopy(ang[:], tb_w[:])
        act(sa[:], sa[:], AF.Sin)
        act(ca[:], ca[:], AF.Sin)
        PW = t64("PW", 9 * 64).rearrange("p (k c g) -> p k c g", k=9, c=2)
        vmemset(PW[:, 0, 0, :], 1.0)
        vmemset(PW[:, 0, 1, :], 0.0)
        tt(PW[:, 1, 0, :], mag[:], ca[:], ALU.mult)
        tt(PW[:, 1, 1, :], mag[:], sa[:], ALU.mult)
        ta = t64("ta", 32); tb = t64("tb", 32)
        for k in range(2, 9):
            tt(ta[:], PW[:, k - 1, 0, :], PW[:, 1, 0, :], ALU.mult)
            tt(tb[:], PW[:, k - 1, 1, :], PW[:, 1, 1, :], ALU.mult)
            tt(PW[:, k, 0, :], ta[:], tb[:], ALU.subtract)
            tt(ta[:], PW[:, k - 1, 0, :], PW[:, 1, 1, :], ALU.mult)
            tt(tb[:], PW[:, k - 1, 1, :], PW[:, 1, 0, :], ALU.mult)
            tt(PW[:, k, 1, :], ta[:], tb[:], ALU.add)
        vcopy(ARt[:, 0:32], PW[:, 8, 0, :])
        vcopy(ARt[:, 32:64], PW[:, 8, 0, :])
        ts(AIt[:, 0:32], PW[:, 8, 1, :], -1.0, None, ALU.mult)
        vcopy(AIt[:, 32:64], PW[:, 8, 1, :])
        QW = t64("QW", (MB + 1) * 64).rearrange("p (k c g) -> p k c g", k=MB + 1, c=2)
        vcopy(QW[:, 1, 0, :], PW[:, 8, 0, :])
        vcopy(QW[:, 1, 1, :], PW[:, 8, 1, :])
        for k in range(2, MB + 1):
            tt(ta[:], QW[:, k - 1, 0, :], QW[:, 1, 0, :], ALU.mult)
            tt(tb[:], QW[:, k - 1, 1, :], QW[:, 1, 1, :], ALU.mult)
            tt(QW[:, k, 0, :], ta[:], tb[:], ALU.subtract)
            tt(ta[:], QW[:, k - 1, 0, :], QW[:, 1, 1, :], ALU.mult)
            tt(tb[:], QW[:, k - 1, 1, :], QW[:, 1, 0, :], ALU.mult)
            tt(QW[:, k, 1, :], ta[:], tb[:], ALU.add)
        for k in range(1, MB + 1):
            ts(AIJ[:, k - 1, 0:32], QW[:, k, 1, :], -1.0, None, ALU.mult)
            vcopy(AIJ[:, k - 1, 32:64], QW[:, k, 1, :])
        sm["QWv"] = QW
        nr = t64("nr", 32); den = t64("den", 32); cr = t64("cr", 32); ci = t64("ci", 32)
        ts(nr[:], PW[:, 1, 0, :], 1.0, None, ALU.subtract)
        tt(den[:], LR[:], LR[:], ALU.mult)
        tt(ta[:], LI[:], LI[:], ALU.mult)
        tt(den[:], den[:], ta[:], ALU.add)
        S.op("dve", lambda: nc.vector.reciprocal(den[:], den[:]), reads=[den[:]], writes=[den[:]])
        tt(ta[:], nr[:], LR[:], ALU.mult)
        tt(tb[:], PW[:, 1, 1, :], LI[:], ALU.mult)
        tt(cr[:], ta[:], tb[:], ALU.add)
        tt(cr[:], cr[:], den[:], ALU.mult)
        tt(ta[:], PW[:, 1, 1, :], LR[:], ALU.mult)
        tt(tb[:], nr[:], LI[:], ALU.mult)
        tt(ci[:], ta[:], tb[:], ALU.subtract)
        tt(ci[:], ci[:], den[:], ALU.mult)
        Wc = t64("Wc", 8 * 64).rearrange("p (k c g) -> p k c g", k=8, c=2)
        tk = t64("tk", 8 * 32).rearrange("p (k g) -> p k g", k=8)
        tk2 = t64("tk2", 8 * 32).rearrange("p (k g) -> p k g", k=8)
        crb = cr[:].unsqueeze(1).broadcast_to([64, 8, 32])
        cib = ci[:].unsqueeze(1).broadcast_to([64, 8, 32])
        tt(tk, PW[:, 0:8, 0, :], crb, ALU.mult)
        tt(tk2, PW[:, 0:8, 1, :], cib, ALU.mult)
        tt(Wc[:, :, 0, :], tk, tk2, ALU.subtract)
        tt(tk, PW[:, 0:8, 0, :], cib, ALU.mult)
        tt(tk2, PW[:, 0:8, 1, :], crb, ALU.mult)
        tt(Wc[:, :, 1, :], tk, tk2, ALU.add)
        BR = av(45056, 2048, F32).rearrange("p (g j) -> p g j", g=32)
        BI = av(47104, 2048, F32).rearrange("p (g j) -> p g j", g=32)
        CR = av(49152, 2048, F32).rearrange("p (g i) -> p g i", g=32)
        CI = av(51200, 2048, F32).rearrange("p (g i) -> p g i", g=32)
        dma("sp", BR[0:64], b_re.rearrange("g p j -> p g j"))
        dma("sp", BI[0:64], b_im.rearrange("g p j -> p g j"))
        ct = av(53248, 2048, F32).rearrange("p (c q) -> p c q", c=4)
        dma("sp", ct[:, :, 0:64], c_re.rearrange("(c gl) i p -> (gl i) c p", c=4))
        dma("sp", ct[:, :, 64:128], c_im.rearrange("(c gl) i p -> (gl i) c p", c=4))
        bk1 = bank(); bk2 = bank()
        for c4 in range(4):
            tr(bk1[0:64, c4 * 128:(c4 + 1) * 128], ct[:, c4, 0:64], identf[:])
            tr(bk2[0:64, c4 * 128:(c4 + 1) * 128], ct[:, c4, 64:128], identf[:])
        vcopy(CR[0:64], bk1[0:64, :].rearrange("p (g i) -> p g i", g=32))
        vcopy(CI[0:64], bk2[0:64, :].rearrange("p (g i) -> p g i", g=32))
        tmpa = av(55296, 1024, F32).rearrange("p (g j) -> p g j", g=16)
        tmpb = av(56320, 1024, F32).rearrange("p (g j) -> p g j", g=16)
        for gh in range(2):
            g0 = 16 * gh
            MIT = av(0, 16384, F32).rearrange("p (c g s j) -> p c g s j", c=2, g=16, s=8)
            MOx = av(16384, 18432, F32).rearrange("p (c g k i) -> p c g k i", c=2, g=16, k=9)
            for s_ in range(8):
                k = 7 - s_
                wr = Wc[:, k, 0, g0:g0 + 16].unsqueeze(2).broadcast_to([64, 16, 16])
                wi = Wc[:, k, 1, g0:g0 + 16].unsqueeze(2).broadcast_to([64, 16, 16])
                tt(tmpa[0:64], wr, BR[0:64, g0:g0 + 16, :], ALU.mult)
                tt(tmpb[0:64], wi, BI[0:64, g0:g0 + 16, :], ALU.mult)
                tt(MIT[0:64, 0, :, s_, :], tmpa[0:64], tmpb[0:64], ALU.subtract)
                tt(tmpa[0:64], wr, BI[0:64, g0:g0 + 16, :], ALU.mult)
                tt(tmpb[0:64], wi, BR[0:64, g0:g0 + 16, :], ALU.mult)
                tt(MIT[0:64, 1, :, s_, :], tmpa[0:64], tmpb[0:64], ALU.add)
            for k in range(9):
                pr = PW[:, k, 0, g0:g0 + 16].unsqueeze(2).broadcast_to([64, 16, 16])
                pi = PW[:, k, 1, g0:g0 + 16].unsqueeze(2).broadcast_to([64, 16, 16])
                tt(tmpa[0:64], CR[0:64, g0:g0 + 16, :], pr, ALU.mult)
                tt(tmpb[0:64], CI[0:64, g0:g0 + 16, :], pi, ALU.mult)
                tt(MOx[0:64, 0, :, k, :], tmpa[0:64], tmpb[0:64], ALU.subtract)
                tt(tmpa[0:64], CR[0:64, g0:g0 + 16, :], pi, ALU.mult)
                tt(tmpb[0:64], CI[0:64, g0:g0 + 16, :], pr, ALU.mult)
                stt(MOx[0:64, 1, :, k, :], tmpa[0:64], -1.0, tmpb[0:64], ALU.mult, ALU.subtract)
            for c in range(2):
                acopy(MO[:, c, g0:g0 + 16, :].rearrange("p g (k i) -> p g k i", k=8),
                      MOx[0:64, c, :, 1:9, :])
            for gl in range(16):
                if gl % 4 == 0:
                    bk = bank()
                for c in range(2):
                    col = ((gl % 4) * 2 + c) * 64
                    tr(bk[:, col:col + 64], MIT[0:64, c, gl, :, :].rearrange("p s j -> p (s j)"),
                       identf[0:64, 0:64])
                if gl % 4 == 3:
                    acopy(MI[:, g0 + gl - 3:g0 + gl + 1, :, :],
                          bk.rearrange("p (g c q) -> p g c q", g=4, c=2))
            KKs = av(34816, 8192, F32).rearrange("p (g n) -> p g n", g=16)
            for gl in range(16):
                if gl % 4 == 0:
                    bk = bank()
                o = bk[0:16, (gl % 4) * 128:(gl % 4 + 1) * 128]
                for c in range(2):
                    mm(o, MIT[0:64, c, gl, 7, :], MOx[0:64, c, gl, 0:8, :].rearrange("p k i -> p (k i)"),
                       start=(c == 0), stop=(c == 1))
                if gl % 4 == 3:
                    vcopy(KKs[0:16, gl - 3:gl + 1, :], bk[0:16, :].rearrange("p (g n) -> p g n", g=4))
            for gl in range(16):
                g = g0 + gl
                stt(KKs[0:16, gl, 0:16], identf[0:16, 0:16], dcol[0:16, g:g + 1], KKs[0:16, gl, 0:16],
                    ALU.mult, ALU.add)
            if gh == 0:
                vmemset(MT[:].rearrange("p g n -> p (g n)"), 0.0)
            for s_ in range(8):
                dma("pool", MT[s_ * 16:(s_ + 1) * 16, g0:g0 + 16, s_ * 16:128], KKs[0:16, :, 0:(8 - s_) * 16])
        return sm

    SM = s5_setup()

    def load_ln(i):
        dma("sp", lnp[:, 0, :], ln_g[i].partition_broadcast(128))
        dma("sp", lnp[:, 1, :], ln_b[i].partition_broadcast(128))

    lnrot = [0]

    class LNPipe:
        def __init__(self):
            self.q = []

        def add(self, t, eps):
            r = lnrot[0]
            lnrot[0] = (r + 1) % 8
            e = dict(t=t, r=r, age=1, s2a=False, bk=None)
            S.op("dve", lambda: nc.vector.bn_stats(stats[:, r, 0:6], X[:, t, 0:512]),
                 reads=[X[:, t, 0:512]], writes=[stats[:, r, 0:6]])
            S.op("dve", lambda: nc.vector.bn_stats(stats[:, r, 6:12], X[:, t, 512:1024]),
                 reads=[X[:, t, 512:1024]], writes=[stats[:, r, 6:12]])
            S.op("dve", lambda: nc.vector.bn_aggr(mv[:, r, :], stats[:, r, :]),
                 reads=[stats[:, r, :]], writes=[mv[:, r, :]])
            ts(rstd[:, r, 0:1], mv[:, r, 1:2], eps, None, ALU.add)
            ptt(rstd[:, r, 0:1], rstd[:, r, 0:1], neghalf[:], ALU.pow)
            self.q.append(e)

        def pre(self):
            for e in list(self.q):
                t, r = e["t"], e["r"]
                xt = X[:, t, :]
                if e["age"] == 3:
                    acopy(xT[:, :, t * 128:(t + 1) * 128], e["bk"].rearrange("p (k t) -> p k t", k=8))
                    self.q.remove(e)
                elif e["age"] == 2:
                    tt(xt, xt, lnp[:, 0, :], ALU.mult)
                    tt(xt, xt, lnp[:, 1, :], ALU.add)
                    acopy(Xb[:], xt)
                    e["s2a"] = True
                elif e["age"] == 1:
                    ts(rstd[:, r, 1:2], mv[:, r, 0:1], rstd[:, r, 0:1], -1.0, ALU.mult, ALU.mult)
                    act(xt, xt, AF.Identity, bias=rstd[:, r, 1:2], scale=rstd[:, r, 0:1])
                    e["age"] = 2

        def mid(self):
            for e in self.q:
                if e["age"] == 2 and e["s2a"]:
                    bk = bank().bitcast(BF16)
                    for kc in range(8):
                        tr(bk[:, kc * 128:(kc + 1) * 128], Xb[:, kc * 128:(kc + 1) * 128], identb[:])
                    e["bk"] = bk
                    e["age"] = 3

        def flush(self):
            while self.q:
                self.pre()
                self.mid()

        def flush_fns(self):
            for e in list(self.q):
                if e["age"] == 3:
                    t = e["t"]
                    acopy(xT[:, :, t * 128:(t + 1) * 128], e["bk"].rearrange("p (k t) -> p k t", k=8))
                    self.q.remove(e)
            return [self.flush]

    def to_xT(t):
        acopy(Xb[:], X[:, t, :])
        bk = bank().bitcast(BF16)
        for kc in range(8):
            tr(bk[:, kc * 128:(kc + 1) * 128], Xb[:, kc * 128:(kc + 1) * 128], identb[:])
        vcopy(xT[:, :, t * 128:(t + 1) * 128], bk.rearrange("p (k t) -> p k t", k=8))

    def ffn(w_in, w_out, ln_i, hook=None):
        load_ln(ln_i)
        for hf in range(2):
            pipe = LNPipe()
            for pr in range(6):
                fcs = [f for f in (hf * 11 + 2 * pr, hf * 11 + 2 * pr + 1) if f < hf * 11 + 11]
                nf = len(fcs)
                c0 = fcs[0] * 128
                sl = wload([
                    ((lambda sl, nf=nf: sl[:, 0:2048].rearrange("p (k n) -> p k n", k=8)[:, :, 0:nf * 128]),
                     wview(w_in, c0, nf * 128)),
                    ((lambda sl, nf=nf: sl[:, 2048:4096].rearrange("p (k n) -> p k n", k=8)[:, :, 0:nf * 128]),
                     wview(w_in, DFF + c0, nf * 128))])
                gv = sl[:, 0:2048].rearrange("p (k n) -> p k n", k=8)
                uv = sl[:, 2048:4096].rearrange("p (k n) -> p k n", k=8)
                if pr == 2:
                    wov = w_out.rearrange("(f k) d -> k f d", k=128)
                    for q in range(0, 11, 4):
                        n = min(4, 11 - q)
                        dma("pool", WO[:, q:q + n, :], wov[:, hf * 11 + q:hf * 11 + q + n, :])
                for j, fc in enumerate(fcs):
                    fl = fc - hf * 11
                    for thf in range(2):
                        bg = bank(); bu = bank()
                        for kc in range(8):
                            mm(bg, gv[:, kc, j * 128:(j + 1) * 128], xT[:, kc, thf * 512:(thf + 1) * 512],
                               start=(kc == 0), stop=(kc == 7))
                        for kc in range(8):
                            mm(bu, uv[:, kc, j * 128:(j + 1) * 128], xT[:, kc, thf * 512:(thf + 1) * 512],
                               start=(kc == 0), stop=(kc == 7))
                        rr = (fl * 2 + thf) % 2
                        act(sgt[:, rr, :], bg, AF.Silu)
                        tt(gT[:, fl, thf * 512:(thf + 1) * 512], bu, sgt[:, rr, :], ALU.mult)
                sfree(sl)
            if hf == 1 and hook is not None:
                hook()
            for t in range(8):
                if hf == 1:
                    pipe.pre()
                bks = [bank(), bank()]
                for dh in range(2):
                    for fl in range(11):
                        mm(bks[dh], gT[:, fl, t * 128:(t + 1) * 128], WO[:, fl, dh * 512:(dh + 1) * 512],
                           start=(fl == 0), stop=(fl == 10))
                if hf == 1:
                    pipe.mid()
                for dh in range(2):
                    xs = X[:, t, dh * 512:(dh + 1) * 512]
                    if hf == 0:
                        stt(xs, xs, 2.0 * ALPHA, bks[dh], ALU.mult, ALU.add)
                    else:
                        tt(xs, xs, bks[dh], ALU.add)
                if hf == 1:
                    pipe.add(t, 4.0 * EPS)
            pipe.flush()

    def mix_half(h, first, deferred=()):
        T0 = 4 * h
        c0 = h * 512
        s_za = v8(wload([(v8, wview(wmi, 0, 512))]))
        s_zu = v8(wload([(v8, wview(wmi, 512, 512))]))
        s_zv = v8(wload([(v8, wview(wmi, 1024, 512))]))
        for s_ in range(8):
            bk = bank()
            for kc in range(8):
                mm(bk[0:64, :], xT[:, kc, c0 + s_:c0 + 512:8], s_za[:, kc, :], start=(kc == 0), stop=(kc == 7))
            acopy(Ub[0:64, :, s_, :], bk[0:64, :].rearrange("p (g j) -> p g j", g=32))
        sfree(s_za)
        for g2 in range(2):
            bk = bank().bitcast(BF16)
            for gl in range(16):
                g = g2 * 16 + gl
                tr(bk[:, gl * 64:(gl + 1) * 64], Ub[0:64, g, :, :].rearrange("p s j -> p (s j)"),
                   identb[0:64, 0:64])
            acopy(UT[:, g2 * 16:(g2 + 1) * 16, :], bk.rearrange("p (g b) -> p g b", g=16))
        for q in range(8):
            bk = bank()
            for gl in range(4):
                g = q * 4 + gl
                for c in range(2):
                    col = (gl * 2 + c) * 64
                    mm(bk[0:64, col:col + 64], MI[:, g, c, :], UT[:, g, :], start=True, stop=True)
            bv = bk[0:64, :].rearrange("p (g c b) -> p g c b", g=4, c=2)
            for c in range(2):
                acopy(XS[0:64, :, c, q * 4:q * 4 + 4].rearrange("p b g -> p g b"), bv[:, :, c, :])
        for cc in range(4):
            bk = bank()
            for kc in range(8):
                mm(bk, s_zu[:, kc, cc * 128:(cc + 1) * 128], xT[:, kc, c0:c0 + 512], start=(kc == 0), stop=(kc == 7))
            act(uT[:, cc, :], bk, AF.Gelu)
        sfree(s_zu)
        for q in range(2):
            s_ga = v8(wload([(v8, wview(wmi, 1536 + q * 512, 512))]))
            for dl in range(4):
                dc = q * 4 + dl
                bk = bank()
                for kc in range(8):
                    mm(bk, s_ga[:, kc, dl * 128:(dl + 1) * 128], xT[:, kc, c0:c0 + 512], start=(kc == 0), stop=(kc == 7))
                act(mT[:, dc, :], bk, AF.Sigmoid)
            sfree(s_ga)
        QW = SM["QWv"]
        if first:
            vmemset(carry[:], 0.0)
        KS = 64 // MB
        XSv = XS[0:64]
        rt1 = av(45056, 4096, F32).rearrange("p (k n) -> p k n", k=16)

        def arv(j, K):
            return QW[:, j, 0, :].unsqueeze(1).unsqueeze(1).broadcast_to([64, K, 2, 32])

        def aiv(j, K):
            return AIJ[:, j - 1, :].rearrange("p (c g) -> p c g", c=2).unsqueeze(1).broadcast_to([64, K, 2, 32])

        def cmul_add(dst, src, j, K):
            t1 = rt1[0:64, 0:K, :].rearrange("p k (c g) -> p k c g", c=2)
            t2 = rt2[:, 0:K, :].rearrange("p k (c g) -> p k c g", c=2)
            tt(t1, arv(j, K), src, ALU.mult)
            tt(t2, aiv(j, K), src[:, :, ::-1, :], ALU.mult)
            tt(dst, dst, t1, ALU.add)
            tt(dst, dst, t2, ALU.add)

        def rec_prefix():
            for i in range(1, MB):
                for k0 in range(0, KS, 16):
                    K = min(16, KS - k0)
                    src = XSv[:, (k0 * MB + i - 1):(k0 + K) * MB:MB, :, :]
                    dst = XSv[:, (k0 * MB + i):(k0 + K) * MB:MB, :, :]
                    cmul_add(dst, src, 1, K)

        def rec_seq(k0, k1):
            ar = QW[:, MB, 0, :].unsqueeze(1).broadcast_to([64, 2, 32])
            ai = AIJ[:, MB - 1, :].rearrange("p (c g) -> p c g", c=2)
            t1 = T1s[:].rearrange("p (c g) -> p c g", c=2)
            t2 = T2s[:].rearrange("p (c g) -> p c g", c=2)
            for k in range(k0, k1):
                prev = carry[:].rearrange("p (c g) -> p c g", c=2) if k == 0 else XSv[:, k * MB - 1, :, :]
                cur_ = XSv[:, k * MB + MB - 1, :, :]
                tt(t1, ar, prev, ALU.mult)
                tt(t2, ai, prev[:, ::-1, :], ALU.mult)
                tt(t1, t1, cur_, ALU.add)
                tt(cur_, t1, t2, ALU.add)

        def rec_fill():
            for i in range(MB - 1):
                j = i + 1
                cmul_add(XSv[:, i:i + 1, :, :], carry[:].rearrange("p (k c g) -> p k c g", k=1, c=2), j, 1)
                for k0 in range(1, KS, 16):
                    K = min(16, KS - k0)
                    src = XSv[:, (k0 * MB - 1):(k0 + K) * MB - 1:MB, :, :]
                    dst = XSv[:, (k0 * MB + i):(k0 + K) * MB:MB, :, :]
                    cmul_add(dst, src, j, K)
            acopy(Hb[0:64, 0, :, 1:65], XSv[:, :, 0, :].rearrange("p b g -> p g b"))
            vcopy(Hb[0:64, 1, :, 1:65], XSv[:, :, 1, :].rearrange("p b g -> p g b"))
            acopy(Hb[0:64, :, :, 0], carry[:].rearrange("p (c g) -> p c g", c=2))
            vcopy(carry[:].rearrange("p (c g) -> p c g", c=2), XSv[:, 63, :, :])

        rec_plan = [rec_prefix, (lambda: rec_seq(0, KS // 2)), (lambda: rec_seq(KS // 2, KS)), rec_fill]

        def rec_steps(b0, b1):
            rec_plan[b0 // 16]()

        for fn_ in deferred:
            fn_()
        def gmlp_tile(tl):
            t = T0 + tl
            bk = bank()
            for kc in range(8):
                mm(bk, xT[:, kc, t * 128:(t + 1) * 128], s_zv[:, kc, :], start=(kc == 0), stop=(kc == 7))
            r = tl % 2
            act(zv[:, r, :], bk, AF.Gelu)
            lr = lnrot[0]
            lnrot[0] = (lr + 1) % 8
            S.op("dve", (lambda r=r, lr=lr: nc.vector.bn_stats(stats[:, lr, 0:6], zv[:, r, :])),
                 reads=[zv[:, r, :]], writes=[stats[:, lr, 0:6]])
            S.op("dve", (lambda lr=lr: nc.vector.bn_aggr(mv[:, lr, :], stats[:, lr, 0:6])),
                 reads=[stats[:, lr, 0:6]], writes=[mv[:, lr, :]])
            ts(rstd[:, lr, 0:1], mv[:, lr, 1:2], EPS, None, ALU.add)
            ptt(rstd[:, lr, 0:1], rstd[:, lr, 0:1], neghalf[:], ALU.pow)
            ts(zv[:, r, :], zv[:, r, :], mv[:, lr, 0:1], rstd[:, lr, 0:1], ALU.subtract, ALU.mult)
            tt(zv[:, r, :], zv[:, r, :], gbc[:, 0, :], ALU.mult)
            z4 = zv[:, r, :].rearrange("p (q e d) -> p q e d", q=4, e=2)
            b4 = gbc[:, 1, :].rearrange("p (q e d) -> p q e d", q=4, e=2)
            A_ = ABv[:, r, :].rearrange("p (q e d) -> p q e d", q=4, e=2)
            B_ = ABv[:, 2 + r, :].rearrange("p (q e d) -> p q e d", q=4, e=2)
            tt(A_[:, :, 0, :], z4[:, :, 0, :], b4[:, :, 0, :], ALU.add)
            tt(B_[:, :, 1, :], z4[:, :, 1, :], b4[:, :, 1, :], ALU.add)
            bk = bank()
            for cc in range(4):
                o = bk[:, cc * 128:(cc + 1) * 128]
                mm(o, ABv[:, r, cc * 128:(cc + 1) * 128], WS[:, 2 * cc, :], start=True, stop=False)
                mm(o, ABv[:, 2 + r, cc * 128:(cc + 1) * 128], WS[:, 2 * cc + 1, :], start=False, stop=False)
                mm(o, hsel[0:40, cc * 128:(cc + 1) * 128], bsT[0:40, :], start=False, stop=True)
            tt(ybpre[:, :, tl * 128:(tl + 1) * 128], bk.rearrange("p (c t) -> p c t", c=4),
               uT[:, :, tl * 128:(tl + 1) * 128], ALU.mult)

        s_upb = v4(wload([(v4, up_b.rearrange("(kc k) n -> k kc n", k=128))]))

        def yb_part(dcs):
            for dc in dcs:
                if dc % 4 == 0:
                    s_gb[0] = v8(wload([(v8, wview(wmi, 2560 + dc * 128, 512))]))
                b_yb = bank(); b_gb = bank()
                for kc in range(4):
                    mm(b_yb, s_upb[:, kc, dc * 128:(dc + 1) * 128], ybpre[:, kc, :], start=(kc == 0), stop=(kc == 3))
                o = (dc % 4) * 128
                for kc in range(8):
                    mm(b_gb, s_gb[0][:, kc, o:o + 128], xT[:, kc, c0:c0 + 512], start=(kc == 0), stop=(kc == 7))
                act(zv[:, dc % 2, :], b_gb, AF.Sigmoid)
                tt(m2[:, dc, :], b_yb, zv[:, dc % 2, :], ALU.mult)
                if dc % 4 == 3:
                    sfree(s_gb[0])

        s_gb = [None]
        rec_steps(0, 16)
        gmlp_tile(0)
        gmlp_tile(1)
        rec_steps(16, 32)
        gmlp_tile(2)
        gmlp_tile(3)
        sfree(s_zv)
        rec_steps(32, 48)
        yb_part(range(0, 4))
        rec_steps(48, 64)
        yb_part(range(4, 8))
        sfree(s_upb)
        for q in range(8):
            bk = bank()
            for gl in range(4):
                g = q * 4 + gl
                o = bk[0:64, gl * 128:(gl + 1) * 128]
                mm(o, UT[:, g, :], MT[:, g, :], start=True, stop=False)
                mm(o, Hb[0:64, 0, g, 0:64], MO[:, 0, g, :], start=False, stop=False)
                mm(o, Hb[0:64, 1, g, 0:64], MO[:, 1, g, :], start=False, stop=True)
            act(yg[0:64, :, q * 64:(q + 1) * 64].rearrange("p s (g i) -> p s g i", g=4),
                bk[0:64, :].rearrange("p (g s i) -> p s g i", g=4, s=8), AF.Gelu)
        for c2 in range(2):
            bk = bank().bitcast(BF16)
            for ci_ in range(2):
                cc = c2 * 2 + ci_
                for s_ in range(8):
                    tr(bk[:, ci_ * 512 + s_ * 64:ci_ * 512 + s_ * 64 + 64], yg[0:64, s_, cc * 128:(cc + 1) * 128],
                       identb[0:64, 0:64])
            acopy(ygT[:, c2 * 2:c2 * 2 + 2, :].rearrange("p c (b s) -> p c s b", s=8),
                  bk.rearrange("p (c s b) -> p c s b", c=2, s=8))
        vglu = (lambda sl: sl[:, 0:2048].rearrange("p (k n) -> p k n", k=4))
        s_glu = vglu(wload([(vglu, glu_w.rearrange("(kc k) n -> k kc n", k=128))]))
        s_upa = v4(wload([(v4, up_a.rearrange("(kc k) n -> k kc n", k=128))]))
        for cc in range(4):
            bk = bank()
            for kc in range(4):
                mm(bk, s_glu[:, kc, cc * 128:(cc + 1) * 128], ygT[:, kc, :], start=(kc == 0), stop=(kc == 3))
            r = cc % 4
            act(mtmp[:, r, :], bk, AF.Sigmoid, bias=glub[:, cc:cc + 1])
            tt(yain[:, cc, :], ygT[:, cc, :], mtmp[:, r, :], ALU.mult)
        sfree(s_glu)
        for dc in range(8):
            b_ya = bank()
            for kc in range(4):
                mm(b_ya, s_upa[:, kc, dc * 128:(dc + 1) * 128], yain[:, kc, :], start=(kc == 0), stop=(kc == 3))
            r = dc % 4
            tt(mtmp[:, r, :], b_ya, mT[:, dc, :], ALU.mult)
            tt(mT[:, dc, :], mtmp[:, r, :], m2[:, dc, :], ALU.add)
        sfree(s_upa)
        s_wo = []
        pipe2 = LNPipe()
        for dh in range(2):
            s_wo.append(v8(wload([(v8, wview(wmo, dh * 512, 512))])))
        for tl in range(4):
            t = T0 + tl
            pipe2.pre()
            bks = [bank(), bank()]
            for dh in range(2):
                for kc in range(8):
                    mm(bks[dh], mT[:, kc, tl * 128:(tl + 1) * 128], s_wo[dh][:, kc, :], start=(kc == 0), stop=(kc == 7))
            pipe2.mid()
            for dh in range(2):
                xs = X[:, t, dh * 512:(dh + 1) * 512]
                stt(xs, xs, ALPHA, bks[dh], ALU.mult, ALU.add)
            pipe2.add(t, EPS)
        sfree(s_wo[0])
        sfree(s_wo[1])
        return pipe2.flush_fns()

    ptall = av(49152, 8192, F32).rearrange("p (t n) -> p t n", t=8)
    pTall = av(57344, 4096, BF16).rearrange("p (t k n) -> p t k n", t=8, k=2)
    pbt2 = av(61440, 1024, BF16).rearrange("p (r n) -> p r n", r=2)
    sgp4 = av(0, 8192, F32).rearrange("p (r n) -> p r n", r=4)

    def ple_prep(st):
        for t in range(8):
            r0 = st * 1024 + t * 128
            dma("sp", ptall[:, t, :], p_d[r0:r0 + 128, :])
        for t4 in range(2):
            bk = bank().bitcast(BF16)
            for tl in range(4):
                t = t4 * 4 + tl
                acopy(pbt2[:, t % 2, :], ptall[:, t, :])
                for kc in range(2):
                    tr(bk[:, (tl * 2 + kc) * 128:(tl * 2 + kc + 1) * 128], pbt2[:, t % 2, kc * 128:(kc + 1) * 128],
                       identb[:])
            vcopy(pTall[:, t4 * 4:t4 * 4 + 4, :, :], bk.rearrange("p (t k n) -> p t k n", t=4, k=2))

    def ple(st):
        s_pg = []
        for dh in range(2):
            s_pg.append(v8(wload([(v8, wview(wpg, dh * 512, 512))])))
        vpp = (lambda sl: sl[:, 0:2048].rearrange("p (k n) -> p k n", k=2))
        s_pp = vpp(wload([(vpp, wpp.rearrange("(kc k) n -> k kc n", k=128))]))
        for t in range(8):
            r0 = st * 1024 + t * 128
            for dh in range(2):
                bg = bank(); bp = bank()
                for kc in range(8):
                    mm(bg, xT[:, kc, t * 128:(t + 1) * 128], s_pg[dh][:, kc, :], start=(kc == 0), stop=(kc == 7))
                for kc in range(2):
                    mm(bp, pTall[:, t, kc, :], s_pp[:, kc, dh * 512:(dh + 1) * 512], start=(kc == 0), stop=(kc == 1))
                r = (t * 2 + dh) % 4
                act(sgp4[:, r, :], bg, AF.Sigmoid)
                tt(sgp4[:, r, :], bp, sgp4[:, r, :], ALU.mult)
                xs = X[:, t, dh * 512:(dh + 1) * 512]
                tt(xs, xs, sgp4[:, r, :], ALU.add)
            dma("sp", out_d[r0:r0 + 128, :], X[:, t, :])
        sfree(s_pg[0]); sfree(s_pg[1]); sfree(s_pp)

    class _Stop(Exception):
        pass

    def stage(name):
        if dbg and dbg.get("stop") == name:
            raise _Stop()

    def emit_all():
      try:
          stage("setup")
          for st in range(nst):
              for t in range(8):
                  r0 = st * 1024 + t * 128
                  dma("sp", X[:, t, :], x_d[r0:r0 + 128, :])
              for t in range(8):
                  to_xT(t)
              stage("xT%d" % st)
              ffn(w1i, w1o, 0)
              stage("ffn1_%d" % st)
              load_ln(1)
              dfr = []
              for h in range(2):
                  dfr = mix_half(h, first=(st % 2 == 0 and h == 0), deferred=dfr)
                  stage("mix%d_%d" % (h, st))
              for fn_ in dfr:
                  fn_()
              ffn(w2i, w2o, 2, hook=(lambda st=st: ple_prep(st)))
              stage("ffn2_%d" % st)
              ple(st)
      except _Stop:
        pass


    class _Dummy:
        def op(self, *a, **k):
            return None

    S_real = S
    saved = (pbank[0], lnrot[0], list(free_slots))
    S = _Dummy()
    S.h = S_real.h
    emit_all()
    pbank[0], lnrot[0] = saved[0], saved[1]
    free_slots[:] = saved[2]
    S = S_real
    W["mode"] = "play"
    pump()
    emit_all()

    if dbg:
        lv = dict(X=X[:], xT=xT[:], MI=MI[:], MT=MT[:], MO=MO[:], WS=WS[:], bsT=bsT[:], ARt=ARt[:], AIt=AIt[:],
                  gT=gT, UT=UT[:, :, :], XS=XS[0:64], Hb=Hb[0:64], yg=yg[0:64], ygT=ygT, yain=yain, uT=uT,
                  ybpre=ybpre, mT=mT, Ub=Ub[0:64], dcol=dcol[:], glub=glub[:])
        for k_, v_ in SM.items():
            lv['s5_' + k_] = v_[:]
        for name in dbg.get("dump", []):
            ap = lv[name]
            dt_ = nc.dram_tensor("dbg_" + name, list(ap.shape), ap.dtype, kind="ExternalOutput").ap()
            dma("sp", dt_, ap)
    print('sbuf bytes remaining', nc.sbuf_bytes_remaining)
    S.emit()
    return nc


_NC = {}


def _consts():
    identf = np.eye(128, dtype=np.float32)
    mask = np.triu(np.ones((128, 128), dtype=np.float32))
    hsel = np.zeros((40, 512), dtype=np.float32)
    for h in range(8):
        hsel[h, h * 64:(h + 1) * 64] = 1.0
        hsel[32 + h, h * 64:(h + 1) * 64] = 1.0
    return dict(c_identf=identf, c_mask=mask, c_hsel=hsel)


def kernel(**inputs):
    n = 8
    if "nc" not in _NC:
        _NC["nc"] = build(4)
    nc = _NC["nc"]
    x = np.ascontiguousarray(np.asarray(inputs["x"], dtype=np.float32))
    p = np.ascontiguousarray(np.asarray(inputs["p"], dtype=np.float32))
    shared = {}
    for k, v in inputs.items():
        if k in ("x", "p"):
            continue
        a = np.asarray(v, dtype=np.float32)
        shared[k] = np.ascontiguousarray(a[0])
    shared.update(_consts())
    in_maps = []
    for c in range(n):
        m = dict(shared)
        m["x"] = x[2 * c:2 * c + 2].reshape(NTOK, D)
        m["p"] = p[0, 2 * c:2 * c + 2].reshape(NTOK, 256)
        in_maps.append(m)
    res = run_bass_kernel_spmd(nc, in_maps, core_ids=list(range(n)))
    out = np.stack([np.asarray(r["out"], dtype=np.float32).reshape(2, 2048, D) for r in res.results], axis=0)
    return out.reshape(16, 2048, D)
```
